# Optimizing a Trainium2 kernel written in Bass

```python
import math
import jax, jax.numpy as jnp
from jax import lax
import numpy as np

D_MODEL = 1024
BATCH = 32
SEQ = 256
DEPTH = 1
DEC_BATCH = 8
DEC_SEQ = 2048
PAST_LEN = 256

GRID_W = 64
N_HEADS = 8
HEAD_DIM = 64
V_DIM = 2 * HEAD_DIM
D_ATTN = N_HEADS * 2 * HEAD_DIM
N_POOL_GROUPS = 4
POOL_WINDOWS = (2, 4, 8, 16)
D_POOL = D_MODEL // 2
POOL_GROUP_DIM = D_POOL // N_POOL_GROUPS
D_FF = 2816
CONV_W = 3
ROPE_BASE = 10000.0
Q_BLOCK = 128
EPS = 1e-6
D_IN = D_POOL + 3 * D_ATTN + 2 * D_MODEL

kernel_name = "hybrid_pool_diffattn_prefix_dit_step"


def rms_norm(x, g):
    xf = x.astype(jnp.float32)
    y = xf * lax.rsqrt(jnp.mean(xf * xf, axis=-1, keepdims=True) + EPS)
    return (y * g.astype(jnp.float32)).astype(x.dtype)


def modulate(x, g, shift, scale):
    return rms_norm(x, g) * (1 + scale[:, None, :]) + shift[:, None, :]


def rope_2d(x):
    n_tok = x.shape[1]
    rows = n_tok // GRID_W
    row = jnp.repeat(jnp.arange(rows, dtype=jnp.float32), GRID_W)
    col = jnp.tile(jnp.arange(GRID_W, dtype=jnp.float32), rows)
    n_freq = HEAD_DIM // 4
    inv = ROPE_BASE ** (-jnp.arange(n_freq, dtype=jnp.float32) / n_freq)
    ang = jnp.stack([row[:, None] * inv, col[:, None] * inv], axis=1)
    cos = jnp.cos(ang)[None, :, None, None]
    sin = jnp.sin(ang)[None, :, None, None]
    xr = x.astype(jnp.float32).reshape(x.shape[:-1] + (2, 2, n_freq))
    x1 = xr[..., 0, :]
    x2 = xr[..., 1, :]
    out = jnp.stack([x1 * cos - x2 * sin, x2 * cos + x1 * sin], axis=-2)
    return out.reshape(x.shape).astype(x.dtype)


def multi_pool(u, w_pool, pool_scale):
    B, S, _ = u.shape
    ug = u.astype(jnp.float32).reshape(B, S, N_POOL_GROUPS, POOL_GROUP_DIM)
    cs = jnp.concatenate([jnp.zeros((B, 1, N_POOL_GROUPS, POOL_GROUP_DIM), jnp.float32),
                          jnp.cumsum(ug, axis=1)], axis=1)
    t = jnp.arange(S)
    outs = []
    for gi, w in enumerate(POOL_WINDOWS):
        lo = jnp.clip(t - w // 2, 0, S)
        hi = jnp.clip(t + (w - w // 2), 0, S)
        csg = cs[:, :, gi]
        s = jnp.take(csg, hi, axis=1) - jnp.take(csg, lo, axis=1)
        cnt = (hi - lo).astype(jnp.float32)[None, :, None]
        outs.append(s / cnt - ug[:, :, gi])
    pooled = jnp.stack(outs, axis=2)
    mixed = jnp.einsum('bsgc,gce->bsge', pooled, w_pool.astype(jnp.float32)).reshape(B, S, D_POOL)
    return (mixed * pool_scale.astype(jnp.float32)).astype(u.dtype)


def diff_attention(q, k, v, lam):
    B, S = q.shape[0], q.shape[1]
    nb = S // Q_BLOCK
    qb = q.reshape(B, nb, Q_BLOCK, N_HEADS, 2, HEAD_DIM).swapaxes(0, 1)
    scale = HEAD_DIM ** -0.5
    vf = v.astype(jnp.float32)

    def block(qi):
        s = jnp.einsum('bqhmd,bkhmd->bhmqk', qi, k).astype(jnp.float32) * scale
        p = jax.nn.softmax(s, axis=-1)
        a = p[:, :, 0] - lam * p[:, :, 1]
        return jnp.einsum('bhqk,bkhe->bqhe', a, vf)

    o = lax.map(block, qb)
    return o.swapaxes(0, 1).reshape(B, S, N_HEADS, V_DIM)


def dwconv3(a, w, b):
    ap = jnp.pad(a, ((0, 0), (1, 1), (0, 0)))
    return ap[:, :-2] * w[0] + ap[:, 1:-1] * w[1] + ap[:, 2:] * w[2] + b


def trunk_layer(x, mod, lam_init, w_in, w_pool, pool_scale, w_proj_a, w_proj_b, w_out,
                lam_q1, lam_k1, lam_q2, lam_k2, subln_w, norm1_w, norm2_w,
                w_ffn_in, ffn_conv_w, ffn_conv_b, w_ffn_out, ctx_k=None, ctx_v=None):
    B, S, _ = x.shape
    shift1, scale1, gate1, shift2, scale2, gate2 = jnp.split(mod, 6, axis=-1)

    h = modulate(x, norm1_w, shift1, scale1)
    proj = h @ w_in
    u, q, k, v, gates = jnp.split(
        proj, [D_POOL, D_POOL + D_ATTN, D_POOL + 2 * D_ATTN, D_POOL + 3 * D_ATTN], axis=-1)
    q = q.reshape(B, S, N_HEADS, 2, HEAD_DIM)
    k = k.reshape(B, S, N_HEADS, 2, HEAD_DIM)
    v = v.reshape(B, S, N_HEADS, V_DIM)

    y_a = multi_pool(u, w_pool, pool_scale) @ w_proj_a

    lam = (jnp.exp(jnp.sum(lam_q1.astype(jnp.float32) * lam_k1.astype(jnp.float32)))
           - jnp.exp(jnp.sum(lam_q2.astype(jnp.float32) * lam_k2.astype(jnp.float32)))
           + lam_init)
    if ctx_k is None:
        attn = diff_attention(q, k, v, lam)
    else:
        q = rope_2d(q)
        k_lat = rope_2d(k)
        k_all = jnp.concatenate([ctx_k.astype(k_lat.dtype), k_lat], axis=1)
        v_all = jnp.concatenate([ctx_v.astype(v.dtype), v], axis=1)
        attn = diff_attention(q, k_all, v_all, lam)
    o = rms_norm(attn, subln_w) * (1 - lam_init)
    y_b = o.reshape(B, S, D_ATTN).astype(x.dtype) @ w_proj_b

    g_a, g_b = jnp.split(jax.nn.sigmoid(gates), 2, axis=-1)
    x = x + gate1[:, None, :] * ((g_a * y_a + g_b * y_b) @ w_out)

    h = modulate(x, norm2_w, shift2, scale2)
    a, up = jnp.split(h @ w_ffn_in, 2, axis=-1)
    a = dwconv3(a, ffn_conv_w, ffn_conv_b)
    x = x + gate2[:, None, :] * ((jax.nn.silu(a) * up) @ w_ffn_out)
    return x, k, v


def setup_inputs(seed: int = 0) -> dict:
    key = jax.random.key(seed)
    ks = jax.random.split(key, 26)
    nrm = jax.random.normal
    f32 = jnp.float32
    return {
        "x_prompt": nrm(ks[0], (BATCH, SEQ, D_MODEL), f32),
        "x_sample": nrm(ks[1], (DEC_BATCH, DEC_SEQ, D_MODEL), f32),
        "cache_k": nrm(ks[2], (DEC_BATCH, DEPTH, PAST_LEN, N_HEADS, 2, HEAD_DIM), f32),
        "cache_v": nrm(ks[3], (DEC_BATCH, DEPTH, PAST_LEN, N_HEADS, V_DIM), f32),
        "c": nrm(ks[4], (DEC_BATCH, D_MODEL), f32),
        "c_ctx": nrm(ks[5], (D_MODEL,), f32),
        "w_ada": nrm(ks[6], (DEPTH, D_MODEL, 6 * D_MODEL), f32) * D_MODEL ** -0.5,
        "b_ada": nrm(ks[7], (DEPTH, 6 * D_MODEL), f32) * 0.01,
        "w_in": nrm(ks[8], (DEPTH, D_MODEL, D_IN), f32) * D_MODEL ** -0.5,
        "w_pool": nrm(ks[9], (DEPTH, N_POOL_GROUPS, POOL_GROUP_DIM, POOL_GROUP_DIM), f32) * POOL_GROUP_DIM ** -0.5,
        "pool_scale": 1.0 + 0.1 * nrm(ks[10], (DEPTH, D_POOL), f32),
        "w_proj_a": nrm(ks[11], (DEPTH, D_POOL, D_MODEL), f32) * D_POOL ** -0.5,
        "w_proj_b": nrm(ks[12], (DEPTH, D_ATTN, D_MODEL), f32) * D_ATTN ** -0.5,
        "w_out": nrm(ks[13], (DEPTH, D_MODEL, D_MODEL), f32) * D_MODEL ** -0.5,
        "lam_q1": nrm(ks[14], (DEPTH, HEAD_DIM), f32) * 0.1,
        "lam_k1": nrm(ks[15], (DEPTH, HEAD_DIM), f32) * 0.1,
        "lam_q2": nrm(ks[16], (DEPTH, HEAD_DIM), f32) * 0.1,
        "lam_k2": nrm(ks[17], (DEPTH, HEAD_DIM), f32) * 0.1,
        "subln_w": 1.0 + 0.1 * nrm(ks[18], (DEPTH, V_DIM), f32),
        "norm1_w": 1.0 + 0.1 * nrm(ks[19], (DEPTH, D_MODEL), f32),
        "norm2_w": 1.0 + 0.1 * nrm(ks[20], (DEPTH, D_MODEL), f32),
        "w_ffn_in": nrm(ks[21], (DEPTH, D_MODEL, 2 * D_FF), f32) * D_MODEL ** -0.5,
        "ffn_conv_w": nrm(ks[22], (DEPTH, CONV_W, D_FF), f32) * CONV_W ** -0.5,
        "ffn_conv_b": nrm(ks[23], (DEPTH, D_FF), f32) * 0.01,
        "w_ffn_out": nrm(ks[24], (DEPTH, D_FF, D_MODEL), f32) * D_FF ** -0.5,
        "final_norm_w": 1.0 + 0.1 * nrm(ks[25], (D_MODEL,), f32),
    }


def reference(x_prompt, x_sample, cache_k, cache_v, c, c_ctx, w_ada, b_ada, w_in, w_pool, pool_scale,
              w_proj_a, w_proj_b, w_out, lam_q1, lam_k1, lam_q2, lam_k2, subln_w, norm1_w, norm2_w,
              w_ffn_in, ffn_conv_w, ffn_conv_b, w_ffn_out, final_norm_w):
    xp = x_prompt
    xs = x_sample
    k_list = []
    v_list = []
    for i in range(DEPTH):
        lam_init = 0.8 - 0.6 * math.exp(-0.3 * i)
        mod_ctx = jax.nn.silu(c_ctx)[None, :] @ w_ada[i] + b_ada[i]
        mod_lat = jax.nn.silu(c) @ w_ada[i] + b_ada[i]
        layer_w = dict(w_in=w_in[i], w_pool=w_pool[i], pool_scale=pool_scale[i], w_proj_a=w_proj_a[i],
                       w_proj_b=w_proj_b[i], w_out=w_out[i], lam_q1=lam_q1[i], lam_k1=lam_k1[i],
                       lam_q2=lam_q2[i], lam_k2=lam_k2[i], subln_w=subln_w[i], norm1_w=norm1_w[i],
                       norm2_w=norm2_w[i], w_ffn_in=w_ffn_in[i], ffn_conv_w=ffn_conv_w[i],
                       ffn_conv_b=ffn_conv_b[i], w_ffn_out=w_ffn_out[i])
        xp, k_ctx, v_ctx = trunk_layer(xp, mod_ctx, lam_init, **layer_w)
        k_list.append(k_ctx)
        v_list.append(v_ctx)
        xs, _, _ = trunk_layer(xs, mod_lat, lam_init, ctx_k=cache_k[:, i], ctx_v=cache_v[:, i], **layer_w)
    y_prompt = rms_norm(xp, final_norm_w)
    y_sample = rms_norm(xs, final_norm_w)
    new_cache_k = jnp.stack(k_list, axis=1)
    new_cache_v = jnp.stack(v_list, axis=1)
    return (y_prompt, y_sample, new_cache_k, new_cache_v)
```

```python
import contextlib
import os


class _Stop(Exception):
    pass


def _chk(tag):
    if os.environ.get('MK_STOP') == tag:
        raise _Stop()

import numpy as np
import concourse.bass as bass
import concourse.mybir as mybir
from concourse.bass_utils import run_bass_kernel_spmd

F32 = mybir.dt.float32
BF16 = mybir.dt.bfloat16
AF = mybir.ActivationFunctionType
ALU = mybir.AluOpType
AX = mybir.AxisListType

D = 1024
NH = 8
DFF = 2816
NFC = 22
DIN = 5632
EPS = 1e-6
LAM_INIT = 0.2
N_CORES = 8


class Stream:
    def __init__(self, name, sem, skip_self):
        self.name = name
        self.ops = []
        self.sem = sem
        self.count = 0
        self.waited = {}
        self.pending = []
        self.last = None
        self.skip_self = skip_self

    def _waits(self, waits):
        w = []
        allw = list(waits) + self.pending
        self.pending = []
        for tok in allw:
            if tok is None:
                continue
            toks = tok if isinstance(tok, list) else [tok]
            for t in toks:
                if t is None:
                    continue
                s, v = t
                if s is self.sem and self.skip_self:
                    continue
                if self.waited.get(id(s), 0) >= v:
                    continue
                self.waited[id(s)] = v
                w.append((s, v))
        return w

    def op(self, fn, waits=(), signal=True):
        w = self._waits(waits)
        tok = None
        if signal:
            self.count += 1
            tok = (self.sem, self.count)
            self.last = tok
        self.ops.append((w, fn, (self.sem, 1) if signal else None))
        return tok

    def dma(self, fn, semc, waits=()):
        w = self._waits(waits)
        semc[1] += 16
        self.ops.append((w, fn, (semc[0], 16)))
        return (semc[0], semc[1])

    def wait_only(self, waits):
        w = self._waits(waits)
        self.ops.append((w, None, None))

    def emit(self, eng):
        for w, fn, inc in self.ops:
            for s, v in w:
                eng.wait_ge(s, v)
            if fn is None:
                continue
            ins = fn(eng)
            if inc is not None:
                ins.then_inc(inc[0], inc[1])


class Dep:
    def __init__(self):
        self.lw = {}
        self.rd = {}

    def get(self, r, w):
        out = []
        for n in r:
            if n in self.lw:
                out.append(self.lw[n])
        for n in w:
            if n in self.lw:
                out.append(self.lw[n])
            out.extend(self.rd.get(n, {}).values())
        return out

    def commit(self, tok, r, w):
        for n in r:
            d = self.rd.setdefault(n, {})
            k = id(tok[0])
            if k not in d or d[k][1] < tok[1]:
                d[k] = tok
        for n in w:
            self.lw[n] = tok
            self.rd[n] = {}


def build_program():
    nc = bass.Bass("TRN2", target_bir_lowering=False)
    dt_in = lambda name, shape: nc.dram_tensor(name, list(shape), F32, kind="ExternalInput")
    dt_out = lambda name, shape: nc.dram_tensor(name, list(shape), F32, kind="ExternalOutput")
    xp = dt_in("xp", [1024, D])
    xs = dt_in("xs", [2048, D])
    ck = dt_in("ck", [256, D])
    cv = dt_in("cv", [256, D])
    cc = dt_in("cc", [2, D])
    w_ada = dt_in("w_ada", [D, 6 * D])
    b_ada = dt_in("b_ada", [6 * D])
    w_in = dt_in("w_in", [D, DIN])
    w_pool = dt_in("w_pool", [4, 128, 128])
    pool_scale = dt_in("pool_scale", [512])
    w_proj_a = dt_in("w_proj_a", [512, D])
    w_proj_b = dt_in("w_proj_b", [D, D])
    w_out = dt_in("w_out", [D, D])
    lamv = dt_in("lamv", [4, 64])
    subln_w = dt_in("subln_w", [128])
    norm1_w = dt_in("norm1_w", [D])
    norm2_w = dt_in("norm2_w", [D])
    w_ffn_in = dt_in("w_ffn_in", [D, 2 * DFF])
    conv_w = dt_in("conv_w", [3, DFF])
    conv_b = dt_in("conv_b", [DFF])
    w_ffn_out = dt_in("w_ffn_out", [DFF, D])
    fnw = dt_in("fnw", [D])
    c_ident = dt_in("c_ident", [128, 128])
    c_cos = dt_in("c_cos", [128, 2048])
    c_sin = dt_in("c_sin", [128, 2048])
    c_ect = dt_in("c_ect", [128, 64])
    c_pm = dt_in("c_pm", [128, 128])
    yp = dt_out("yp", [1024, D])
    ys = dt_out("ys", [2048, D])
    nk = dt_out("nk", [1024, D])
    nv = dt_out("nv", [1024, D])

    es = contextlib.ExitStack()
    with es:
        def sbt(name, nfree, dt=F32):
            return es.enter_context(nc.sbuf_tensor(name, [128, nfree], dt))

        KB = 1024
        RA = sbt("RA", 64 * KB // 4)
        RB = sbt("RB", 32 * KB // 4)
        RC = sbt("RC", 66 * KB // 4)
        RING = sbt("RING", 16 * KB // 4)
        MOD = sbt("MOD", 8 * KB // 4)
        CST = sbt("CST", 9 * KB // 4)
        FS = sbt("FS", 12 * KB // 4)
        psum = es.enter_context(nc.psum_tensor("psum", [128, 8, 512], F32))

        def view(reg, off, shape, dt):
            esz = 2 if dt == BF16 else 4
            n = int(np.prod(shape)) * esz
            assert off % 4 == 0 and n % 4 == 0
            assert off + n <= reg.shape[1] * 4, (off, n, reg.shape)
            ap = reg[:, off // 4:(off + n) // 4]
            if dt != F32:
                ap = ap.bitcast(dt)
            if len(shape) == 2:
                return ap.rearrange("p (a b) -> p a b", a=shape[0])
            if len(shape) == 3:
                return ap.rearrange("p (a b c) -> p a b c", a=shape[0], b=shape[1])
            return ap

        sem_names = ["pe", "act", "dve", "pool", "ring0", "ring1", "xl0", "xl1", "st0", "st1", "st2", "st3", "st4", "cwb", "wsc0", "wsc1", "wsc2", "wsc3", "wsc4", "wsc5", "wsc6", "wsc7", "wsc8", "wsc9", "wsc10", "misc", "miscp", "wfo", "wfo2", "ctxk", "ctxv0", "ctxv1", "mb", "mn"]
        sems = {n: es.enter_context(nc.semaphore(n)) for n in sem_names}
        block = es.enter_context(nc.Block())
        PE = Stream("pe", sems["pe"], True)
        ACT = Stream("act", sems["act"], False)
        DVE = Stream("dve", sems["dve"], False)
        POOL = Stream("pool", sems["pool"], False)
        SP = Stream("sp", None, False)
        dsem = {n: [sems[n], 0] for n in sem_names[4:]}
        dep = Dep()

        def ev(st, fn, r=(), w=(), extra=()):
            tok = st.op(fn, dep.get(r, w) + list(extra))
            dep.commit(tok, r, w)
            return tok

        def peg(fns, r=(), w=(), extra=()):
            waits = dep.get(r, w) + list(extra)
            tok = None
            n = len(fns)
            for i, fn in enumerate(fns):
                tok = PE.op(fn, waits if i == 0 else (), signal=(i == n - 1))
            dep.commit(tok, r, w)
            return tok

        def dm(st, fns, semc, r=(), w=(), extra=()):
            waits = dep.get(r, w) + list(extra)
            tok = None
            for fn in fns:
                tok = st.dma(fn, semc, waits)
            dep.commit(tok, r, w)
            return tok

        def barrier():
            toks = [PE.last, ACT.last, DVE.last]
            for s in (PE, ACT, DVE, SP):
                s.pending.extend(toks)
            return toks

        o = 0

        def cst(shape, dt):
            nonlocal o
            esz = 2 if dt == BF16 else 4
            n = int(np.prod(shape)) * esz
            n = (n + 3) // 4 * 4
            v = view(CST, o, shape, dt)
            o += n
            return v
        ident = cst([128], BF16)
        zeros = cst([128], F32)
        csil = cst([2, 8, 128], BF16)
        craw = cst([2, 8], F32)
        swb = cst([128], F32)
        lamb = cst([4, 64], F32)
        lamt = cst([8], F32)
        pscale = cst([4], F32)
        cw = cst([3, NFC], F32)
        cb = cst([NFC], F32)
        ect = cst([64], F32)
        wpool = cst([4, 128], BF16)
        ss_t = cst([16], F32)
        rs_t = cst([16], F32)
        att_s = cst([2, 8], F32)
        Pm = cst([128], BF16)
        epsb = cst([1], F32)
        assert o <= 9 * KB, o

        misc = dsem["misc"]
        miscp = dsem["miscp"]
        dm(POOL, [lambda e: e.dma_start(out=ident, in_=c_ident.ap()),
                  lambda e: e.dma_start(out=wpool, in_=w_pool.ap().rearrange("g c e -> c g e")),
                  lambda e: e.dma_start(out=Pm, in_=c_pm.ap())], miscp, w=["ident", "wpool", "Pm"])
        dm(SP, [lambda e: e.dma_start(out=craw, in_=cc.ap().rearrange("a (k p) -> p a k", p=128), allow_slow_non_contiguous=True),
                lambda e: e.dma_start(out=swb, in_=bass.AP(subln_w, 0, [[0, 128], [1, 128]])),
                lambda e: e.dma_start(out=lamb.rearrange("p a b -> p (a b)"), in_=bass.AP(lamv, 0, [[0, 128], [1, 256]])),
                lambda e: e.dma_start(out=ect, in_=c_ect.ap())], misc,
           w=["craw", "swb", "lamb", "ect"])

        def late_param_loads():
            dm(SP, [lambda e: e.dma_start(out=pscale, in_=pool_scale.ap().rearrange("(g p) -> p g", p=128), allow_slow_non_contiguous=True),
                    lambda e: e.dma_start(out=cw, in_=conv_w.ap().rearrange("j (f p) -> p j f", p=128), allow_slow_non_contiguous=True),
                    lambda e: e.dma_start(out=cb, in_=conv_b.ap().rearrange("(f p) -> p f", p=128), allow_slow_non_contiguous=True)], dsem["cwb"],
               w=["pscale", "cw", "cb"])

        ev(DVE, lambda e: e.memset(zeros, 0.0), w=["zeros"])
        ev(DVE, lambda e: e.memset(epsb, EPS), w=["epsb"])
        ev(ACT, lambda e: e.activation(out=craw, in_=craw, func=AF.Silu), r=["craw"], w=["craw"])
        for a in range(2):
            for k in range(8):
                ev(DVE, lambda e, a=a, k=k: e.tensor_scalar(out=csil[:, a, k, :], in0=zeros, scalar1=craw[:, a, k:k + 1], scalar2=None, op0=ALU.add),
                   r=["zeros", "craw"], w=["csil%d_%d" % (a, k)])
        ev(ACT, lambda e: e.activation(out=swb, in_=swb, func=AF.Copy, scale=1.0 - LAM_INIT), r=["swb"], w=["swb"])
        ev(DVE, lambda e: e.tensor_tensor(out=lamb[:, 0, :], in0=lamb[:, 0, :], in1=lamb[:, 1, :], op=ALU.mult), r=["lamb"], w=["lamb"])
        ev(DVE, lambda e: e.tensor_tensor(out=lamb[:, 2, :], in0=lamb[:, 2, :], in1=lamb[:, 3, :], op=ALU.mult), r=["lamb"], w=["lamb"])
        ev(DVE, lambda e: e.tensor_reduce(out=lamt[:, 0:1], in_=lamb[:, 0, :], axis=AX.X, op=ALU.add), r=["lamb"], w=["lamt"])
        ev(DVE, lambda e: e.tensor_reduce(out=lamt[:, 1:2], in_=lamb[:, 2, :], axis=AX.X, op=ALU.add), r=["lamb"], w=["lamt"])
        ev(ACT, lambda e: e.activation(out=lamt[:, 2:4], in_=lamt[:, 0:2], func=AF.Exp), r=["lamt"], w=["lamt"])
        ev(DVE, lambda e: e.tensor_tensor(out=lamt[:, 4:5], in0=lamt[:, 2:3], in1=lamt[:, 3:4], op=ALU.subtract), r=["lamt"], w=["lamt"])
        ev(DVE, lambda e: e.tensor_scalar(out=lamt[:, 5:6], in0=lamt[:, 4:5], scalar1=LAM_INIT, scalar2=-1.0, op0=ALU.add, op1=ALU.mult), r=["lamt"], w=["lamt"])
        neglam = lamt[:, 5:6]
        CSIL = ["csil%d_%d" % (a, k) for a in range(2) for k in range(8)]

        ring_views = [view(RING, 0, [4096], BF16), view(RING, 8 * KB, [4096], BF16)]
        ring_i = [0]

        def ring_load(pieces, extra=(), r=()):
            i = ring_i[0] % 2
            ring_i[0] += 1
            sv = ring_views[i]
            fns = [(lambda e, d=dst_fn(sv), s=src: e.dma_start(out=d, in_=s)) for dst_fn, src in pieces]
            dm(POOL, fns, dsem["ring%d" % i], r=r, w=["ring%d" % i], extra=extra)
            return sv, "ring%d" % i

        def wsrc(wt, c0, ncols, k0=0, nk=8):
            return wt.ap()[k0 * 128:(k0 + nk) * 128, c0:c0 + ncols].rearrange("(k p) c -> p k c", p=128)

        def slab3(sv, off, nk, ncols):
            return sv[:, off:off + nk * ncols].rearrange("p (k c) -> p k c", k=nk)

        modv = [view(MOD, 0, [1024], F32), view(MOD, 4 * KB, [1024], F32)]
        modscr = nc.dram_tensor("modscr", [2, 6 * D], F32, kind="Internal")
        wfi_scr = nc.dram_tensor("wfi_scr", [NFC // 2, 128, 4096], BF16, kind="Internal")

        def prefetch_wfi(fc2):
            dst = wfi_scr.ap()[fc2].rearrange("p (k c) -> p k c", k=8)
            dm(POOL, [lambda e: e.dma_start(out=dst[:, :, 0:256], in_=wsrc(w_ffn_in, fc2 * 256, 256)),
                      lambda e: e.dma_start(out=dst[:, :, 256:512], in_=wsrc(w_ffn_in, DFF + fc2 * 256, 256))], dsem["wsc%d" % fc2], w=["wsc%d" % fc2])
        csil2 = cst([8, 2], BF16)
        ev(DVE, lambda e: e.tensor_copy(out=csil2, in_=craw.rearrange("p a k -> p k a")), r=["craw"], w=["csil2"])
        mr_bb = view(RC, 48 * KB, [1024], F32)
        mr_nw = view(RC, 52 * KB, [1024], F32)
        mr_row = [view(RC, 56 * KB + b * 2 * KB, [512], F32) for b in range(2)]
        NORMW = {1: norm1_w, 4: norm2_w}
        mri = [0]

        def start_modrows(j, use_ring=True, extra=()):
            dm(SP, [lambda e: e.dma_start(out=mr_bb[0:2, :], in_=bass.AP(b_ada, j * 1024, [[0, 2], [1, 1024]]))], dsem["mb"], w=["mr_bb"])
            if j in NORMW:
                dm(SP, [lambda e: e.dma_start(out=mr_nw[0:2, :], in_=bass.AP(NORMW[j], 0, [[0, 2], [1, 1024]]))], dsem["mn"], w=["mr_nw"])
            slabs = []
            for hf in range(2):
                if use_ring:
                    slabs.append(ring_load([(lambda s: slab3(s, 0, 8, 512), wsrc(w_ada, j * 1024 + hf * 512, 512))]))
                else:
                    sv = view(RC, 16 * KB + hf * 8 * KB, [4096], BF16)
                    nm = "mslab%d" % hf
                    dm(POOL, [lambda e, sv=sv, hf=hf: e.dma_start(out=slab3(sv, 0, 8, 512), in_=wsrc(w_ada, j * 1024 + hf * 512, 512))], dsem["ctxk" if hf == 0 else "ctxv0"], w=[nm], extra=extra)
                    slabs.append((sv, nm))
            return (j, slabs)

        def finish_modrows(st_):
            j, slabs = st_
            for hf in range(2):
                sv, rn = slabs[hf]
                wv = slab3(sv, 0, 8, 512)
                bank = hf
                rb = mri[0] % 2
                mri[0] += 1
                row = mr_row[rb]
                peg([(lambda e, k=k, wv=wv, bank=bank: e.matmul(psum[0:2, bank, :], lhsT=csil2[:, k, :], rhs=wv[:, k, :], start=(k == 0), stop=(k == 7))) for k in range(8)],
                    r=[rn, "csil2"], w=["p%d" % bank])
                sl = slice(hf * 512, hf * 512 + 512)
                ev(DVE, lambda e, sl=sl, bank=bank, row=row: e.tensor_tensor(out=row[0:2, :], in0=psum[0:2, bank, :], in1=mr_bb[0:2, sl], op=ALU.add), r=["mr_bb"], w=["p%d" % bank, "mr_row%d" % rb])
                if j in NORMW:
                    ev(DVE, lambda e, sl=sl, row=row: e.scalar_tensor_tensor(out=row[0:2, :], in0=row[0:2, :], scalar=1.0, in1=mr_nw[0:2, sl], op0=ALU.add, op1=ALU.mult), r=["mr_nw"], w=["mr_row%d" % rb])
                dm(SP, [lambda e, row=row, hf=hf: e.dma_start(out=modscr.ap()[:, j * 1024 + hf * 512:j * 1024 + (hf + 1) * 512], in_=row[0:2, :])], dsem["st3" if rb == 0 else "st4"], r=["mr_row%d" % rb], w=["modscr%d_%d" % (j, hf)])

        def compute_modrows(j):
            finish_modrows(start_modrows(j))

        def load_mod(cond, j, mi, extra=()):
            dm(SP, [lambda e: e.dma_start(out=modv[mi], in_=bass.AP(modscr, cond * 6 * D + j * 1024, [[0, 128], [1, 1024]]))], dsem["mb" if mi == 0 else "mn"],
               r=["modscr%d_0" % j, "modscr%d_1" % j], w=["mod%d" % mi], extra=extra)

        def norm_preload(src_dram, reg, stage_off, n):
            for i in range(n):
                b = i % 2
                xlb = view(reg, stage_off + b * 4 * KB, [1024], F32)
                dm(SP, [lambda e, i=i, xlb=xlb: e.dma_start(out=xlb, in_=src_dram[i * 128:(i + 1) * 128, :])], dsem["xl%d" % b], w=["xl%d" % b])

        def norm_phase(T, src_dram, src_sb, mS, mG, hT, reg, stage_off, hname, stage=None, pre=0):
            nt = T // 128
            xl = [view(reg, stage_off + 0, [1024], F32), view(reg, stage_off + 4 * KB, [1024], F32)]
            tm = [view(reg, stage_off + 8 * KB, [1024], F32), view(reg, stage_off + 12 * KB, [1024], F32)]
            hb = [view(reg, stage_off + 16 * KB, [1024], BF16), view(reg, stage_off + 18 * KB, [1024], BF16)]
            junk = view(reg, stage_off + 20 * KB, [1024], BF16)
            if stage is not None:
                tm, hb, junk = stage
            Sb, Gb = modv[mS], modv[mG]

            def src(i):
                b = i % 2
                if src_dram is not None:
                    return xl[b], ["xl%d" % b]
                return src_sb[:, i, :], ["x1"]

            def stats(i):
                b = i % 2
                if src_dram is not None and i >= pre:
                    dm(SP, [lambda e, i=i, b=b: e.dma_start(out=xl[b], in_=src_dram[i * 128:(i + 1) * 128, :])], dsem["xl%d" % b], w=["xl%d" % b])
                xv, xn = src(i)
                sl_ = slice(i % 16, i % 16 + 1)
                ev(ACT, lambda e, xv=xv, sl_=sl_: e.activation(out=junk, in_=xv, func=AF.Square, accum_out=ss_t[:, sl_]), r=xn, w=["junk", "ss%d" % (i % 16)])
                ev(ACT, lambda e, sl_=sl_: e.activation(out=rs_t[:, sl_], in_=ss_t[:, sl_], func=AF.Sqrt, scale=1.0 / D, bias=EPS), r=["ss%d" % (i % 16)], w=["rs%d" % (i % 16)])

            def mid(i):
                b = i % 2
                xv, xn = src(i)
                sl_ = slice(i % 16, i % 16 + 1)
                ev(DVE, lambda e, sl_=sl_: e.reciprocal(out=rs_t[:, sl_], in_=rs_t[:, sl_]), w=["rs%d" % (i % 16)])
                ev(DVE, lambda e, xv=xv, sl_=sl_, b=b: e.scalar_tensor_tensor(out=tm[b], in0=xv, scalar=rs_t[:, sl_], in1=Gb, op0=ALU.mult, op1=ALU.mult),
                   r=xn + ["rs%d" % (i % 16), "mod%d" % mG], w=["tm%d" % b])
                ev(DVE, lambda e, b=b: e.tensor_tensor(out=hb[b], in0=tm[b], in1=Sb, op=ALU.add), r=["tm%d" % b, "mod%d" % mS], w=["hb%d" % b])

            def fin(i):
                b = i % 2
                bank = 6 + b
                psb = psum[:, bank, :].bitcast(BF16)
                peg([(lambda e, k=k, b=b, psb=psb: e.transpose(out=psb[:, k * 128:(k + 1) * 128], in_=hb[b][:, k * 128:(k + 1) * 128], identity=ident)) for k in range(8)],
                    r=["hb%d" % b, "ident"], w=["p%d" % bank])
                ev(ACT, lambda e, i=i, psb=psb: e.activation(out=hT[:, :, i * 128:(i + 1) * 128], in_=psb.rearrange("p (k t) -> p k t", k=8), func=AF.Copy), w=["p%d" % bank, hname])

            stats(0)
            for i in range(nt):
                if i + 1 < nt:
                    stats(i + 1)
                mid(i)
                fin(i)

        persist = {}

        def run_group(gi, T, nseq, S, ctx, rope, x_dram, y_dram):
            nt = T // 128
            ntb = T // 512
            cond = gi
            hT = view(RB, 0, [8, T], BF16)
            oT = view(RC, 0, [8, T], BF16)
            mergedT = view(RC, 32 * KB, [8, T], BF16)
            x1 = view(RA, 0, [nt, 1024], F32)
            Lk = ctx + S if rope else T
            nkt_all = Lk // 128

            barrier()
            tmpA = view(RA, 0, [1024], F32)
            tmpB = view(RA, 4 * KB, [1024], F32)
            npre = 0
            if gi == 0:
                npre = 2
                norm_preload(x_dram.ap(), RA, 40 * KB, npre)
                compute_modrows(0)
                compute_modrows(1)
            load_mod(cond, 0, 0)
            load_mod(cond, 1, 1)
            norm_phase(T, x_dram.ap(), None, 0, 1, hT, RA, 40 * KB, "hT", pre=npre)
            barrier()
            _chk('A%d' % gi)
            if gi == 0:
                late_param_loads()

            offB = 0
            qTm = []
            for b in range(2):
                qTm.append([view(RA, offB + (2 * b + m) * T * 2, [T], BF16) for m in range(2)])
            offB += 4 * T * 2
            kT = [view(RA, offB + b * Lk * 2, [Lk], BF16) for b in range(2)]
            offB += 2 * Lk * 2
            vhs = ((nkt_all * 129 * 2 + 3) // 4) * 4
            vh = [view(RA, offB + b * vhs, [nkt_all, 129], BF16) for b in range(2)]
            offB += 2 * vhs
            araws = [view(RA, offB + b * 2064, [2, 258], F32).rearrange("p q (m c) -> p q m c", m=2) for b in range(2)]; offB += 4128
            ckb = view(RA, offB, [2, 128], BF16); offB += 512
            pT = [view(RA, offB + b * 1024, [512], BF16) for b in range(4)]; offB += 4 * KB
            ost = [view(RA, offB + b * 512, [2, 128], BF16) for b in range(2)]; offB += 1 * KB
            att = [[view(RA, offB + (3 * s_ + b) * 1024, [2, 128], F32) for b in range(3)] for s_ in range(2)]; offB += 6 * KB
            rt1 = [view(RA, offB + b * 2 * KB, [512], F32) for b in range(2)]; offB += 4 * KB
            rt2 = [view(RA, offB + b * 2 * KB, [512], F32) for b in range(2)]; offB += 4 * KB
            qraw = [view(RA, offB + b * 1024, [512], BF16) for b in range(2)]; offB += 2 * KB
            kvo = None if rope else [view(RA, offB + b * 4 * KB, [8, 128], F32) for b in range(2)]; offB += (0 if rope else 8 * KB)
            assert offB <= 64 * KB, offB
            cosT = view(RC, 32 * KB, [2048], BF16)
            sinT = view(RC, 36 * KB, [2048], BF16)
            bt = [PE.last, DVE.last, ACT.last]
            if rope:
                dm(POOL, [lambda e: e.dma_start(out=cosT, in_=c_cos.ap()), lambda e: e.dma_start(out=sinT, in_=c_sin.ap())], miscp, w=["rope"], extra=bt)
            for b in range(2):
                for m in range(2):
                    ev(DVE, lambda e, b=b, m=m: e.memset(qTm[b][m], 0.0), w=["qT%d" % b])
                ev(DVE, lambda e, b=b: e.memset(vh[b][:, :, 128:129], 1.0), w=["vh%d" % b])
            SCALE = 64 ** -0.5
            pj = [0]
            qbi = [0]
            gstep = [0]
            dseq = [0]
            dq = []

            def run_due(force=False):
                while dq and (force or dq[0][0] <= gstep[0]):
                    dq.pop(0)[2]()

            def peg_split(fns, r, w, nparts):
                n = len(fns)
                per = n // nparts
                parts = []
                for p in range(nparts):
                    def part(p=p):
                        lo = p * per
                        hi = n if p == nparts - 1 else (p + 1) * per
                        tok = None
                        for i in range(lo, hi):
                            tok = PE.op(fns[i], dep.get(r, w) if i == 0 else (), signal=(i == n - 1))
                        if hi == n:
                            dep.commit(tok, r, w)
                    parts.append(part)
                return parts

            def head_parts(h):
                hb_ = h % 2
                qn, kn, vn = "qT%d" % hb_, "kT%d" % hb_, "vh%d" % hb_
                parts = []
                st = {}

                def u0():
                    sv, rn = ring_load([
                        (lambda s: slab3(s, 0, 8, 384)[:, :, 0:128], wsrc(w_in, 512 + h * 128, 128)),
                        (lambda s: slab3(s, 0, 8, 384)[:, :, 128:256], wsrc(w_in, 1536 + h * 128, 128)),
                        (lambda s: slab3(s, 0, 8, 384)[:, :, 256:384], wsrc(w_in, 2560 + h * 128, 128)),
                    ])
                    st["rn"] = rn
                    st["wq"] = slab3(sv, 0, 8, 384)
                    wq = st["wq"]
                    if ctx:
                        dm(POOL, [lambda e: e.dma_start(out=vh[hb_][:, 0:2, 0:128], in_=cv.ap()[:, h * 128:(h + 1) * 128].rearrange("(t p) c -> p t c", p=128))],
                           dsem["ctxv%d" % hb_], w=[vn], extra=bt)
                        dm(POOL, [lambda e: e.dma_start(out=ckb, in_=ck.ap()[:, h * 128:(h + 1) * 128].rearrange("(t p) c -> p t c", p=128))], dsem["ctxk"], w=["ckb"], extra=bt)
                parts.append(u0)
                if ctx:
                    def u1():
                        psb = psum[:, 7, :].bitcast(BF16)
                        peg([(lambda e, t2=t2, psb=psb: e.transpose(out=psb[:, t2 * 128:(t2 + 1) * 128], in_=ckb[:, t2, :], identity=ident)) for t2 in range(2)], r=["ckb", "ident"], w=["p7"])
                        ev(DVE, lambda e, psb=psb: e.tensor_copy(out=kT[hb_][:, 0:256], in_=psb[:, 0:256]), w=["p7", kn])
                    parts.append(u1)
                for which in range(2):
                    for tb in range(ntb):
                        b0 = 5 if rope else 5 + (pj[0] % 2)
                        rb = pj[0] % 2
                        pj[0] += 1
                        cs = slice(tb * 512, (tb + 1) * 512)

                        def mk_main(which=which, cs=cs, b0=b0):
                            wq = st["wq"]
                            return peg_split([(lambda e, k=k: e.matmul(psum[:, b0, :], lhsT=wq[:, k, which * 128:(which + 1) * 128], rhs=hT[:, k, cs], start=(k == 0), stop=(k == 7))) for k in range(8)],
                                             [st["rn"], "hT"], ["p%d" % b0], 2)

                        def mk_perm(which=which, cs=cs, b0=b0):
                            return peg_split([(lambda e, k=k: e.matmul(psum[:, b0 + 1, :], lhsT=wP[:, k, which, :], rhs=hT[:, k, cs], start=(k == 0), stop=(k == 7))) for k in range(8)],
                                             ["wP", "hT"], ["p%d" % (b0 + 1)], 2)
                        holder = {}
                        for pi in range(2):
                            def part(pi=pi, maker=mk_main, holder=holder):
                                if "p" not in holder:
                                    holder["p"] = maker()
                                holder["p"][pi]()
                            parts.append(part)

                        if rope:
                            def evacA(cs=cs, b0=b0, rb=rb):
                                ev(DVE, lambda e: e.tensor_copy(out=qraw[rb], in_=psum[:, b0, :]), w=["p%d" % b0, "qraw%d" % rb])
                                ev(DVE, lambda e: e.tensor_tensor(out=rt1[rb], in0=psum[:, b0, :], in1=cosT[:, cs], op=ALU.mult), r=["rope"], w=["p%d" % b0, "rt1_%d" % rb])
                            parts.append(evacA)

                            def perm(b0=b0, rb=rb):
                                peg([lambda e: e.matmul(psum[:, b0 + 1, :], lhsT=Pm, rhs=qraw[rb], start=True, stop=True)], r=["qraw%d" % rb, "Pm"], w=["p%d" % (b0 + 1)])
                            parts.append(perm)

                            def evacB(which=which, tb=tb, cs=cs, b0=b0, rb=rb):
                                ev(DVE, lambda e: e.tensor_tensor(out=rt2[rb], in0=psum[:, b0 + 1, :], in1=sinT[:, cs], op=ALU.mult), r=["rope"], w=["p%d" % (b0 + 1), "rt2_%d" % rb])
                                if which == 0:
                                    ev(DVE, lambda e: e.tensor_tensor(out=qTm[hb_][0][0:64, cs], in0=rt1[rb][0:64, :], in1=rt2[rb][0:64, :], op=ALU.add), r=["rt1_%d" % rb, "rt2_%d" % rb], w=[qn])
                                    ev(DVE, lambda e: e.tensor_tensor(out=qTm[hb_][1][64:128, cs], in0=rt1[rb][64:128, :], in1=rt2[rb][64:128, :], op=ALU.add), r=["rt1_%d" % rb, "rt2_%d" % rb], w=[qn])
                                else:
                                    kc_ = slice(ctx + tb * 512, ctx + (tb + 1) * 512)
                                    ev(DVE, lambda e: e.tensor_tensor(out=kT[hb_][:, kc_], in0=rt1[rb], in1=rt2[rb], op=ALU.add), r=["rt1_%d" % rb, "rt2_%d" % rb], w=[kn])
                            parts.append(evacB)
                        else:
                            def evac(which=which, cs=cs, b0=b0):
                                if which == 0:
                                    ev(ACT, lambda e: e.activation(out=qTm[hb_][0][0:64, cs], in_=psum[0:64, b0, :], func=AF.Copy), w=["p%d" % b0, qn])
                                    ev(ACT, lambda e: e.activation(out=qTm[hb_][1][64:128, cs], in_=psum[64:128, b0, :], func=AF.Copy), w=["p%d" % b0, qn])
                                else:
                                    ev(ACT, lambda e: e.activation(out=kT[hb_][:, cs], in_=psum[:, b0, :], func=AF.Copy), w=["p%d" % b0, kn])
                            parts.append(evac)
                outs = [(2, vh[hb_], nv)] + ([] if rope else [(1, None, nk)])
                for (wsel, dst_bf, dst_dram) in outs:
                    for t4 in range(nt // 4):
                        b0 = 5 + (pj[0] % 2)
                        pj[0] += 1

                        def mk_v(wsel=wsel, t4=t4, b0=b0):
                            wq = st["wq"]
                            fns = []
                            for tt in range(4):
                                ti = t4 * 4 + tt
                                for k in range(8):
                                    fns.append(lambda e, k=k, ti=ti, tt=tt: e.matmul(psum[:, b0, tt * 128:(tt + 1) * 128], lhsT=hT[:, k, ti * 128:(ti + 1) * 128], rhs=wq[:, k, wsel * 128:(wsel + 1) * 128], start=(k == 0), stop=(k == 7)))
                            return peg_split(fns, [st["rn"], "hT"], ["p%d" % b0], 4)
                        holder = {}
                        for pi in range(4):
                            def part(pi=pi, maker=mk_v, holder=holder):
                                if "p" not in holder:
                                    holder["p"] = maker()
                                holder["p"][pi]()
                            parts.append(part)

                        def evac_v(wsel=wsel, dst_bf=dst_bf, dst_dram=dst_dram, t4=t4, b0=b0):
                            if dst_bf is not None:
                                kt0 = (ctx // 128) + t4 * 4
                                if rope:
                                    ev(DVE, lambda e: e.tensor_copy(out=dst_bf[:, kt0:kt0 + 4, 0:128], in_=psum[:, b0, :].rearrange("p (t c) -> p t c", t=4)), w=["p%d" % b0, vn])
                                else:
                                    ev(ACT, lambda e: e.activation(out=dst_bf[:, kt0:kt0 + 4, 0:128], in_=psum[:, b0, :].rearrange("p (t c) -> p t c", t=4), func=AF.Copy), w=["p%d" % b0, vn])
                            if not rope:
                                ob = kvo[wsel - 1]
                                on = "kvo%d" % (wsel - 1)
                                ev(ACT, lambda e: e.activation(out=ob[:, t4 * 4:(t4 + 1) * 4, :], in_=psum[:, b0, :].rearrange("p (t c) -> p t c", t=4), func=AF.Copy), w=["p%d" % b0, on])
                                if t4 == nt // 4 - 1:
                                    dm(SP, [lambda e: e.dma_start(out=dst_dram.ap()[:, h * 128:(h + 1) * 128].rearrange("(t p) c -> p t c", p=128), in_=ob)],
                                       dsem["st%d" % (wsel - 1)], r=[on])
                        parts.append(evac_v)
                return parts

            pending_parts = []
            for h in range(NH):
                hb_ = h % 2
                qn, kn, vn = "qT%d" % hb_, "kT%d" % hb_, "vh%d" % hb_
                if h == 0:
                    for part in head_parts(h):
                        part()
                else:
                    for part in pending_parts:
                        part()
                pending_parts = head_parts(h + 1) if (h + 1 < NH) else []
                astep = [0]

                mr_state = None
                mr_j = [(h + 1) if (gi == 0 and 1 <= h <= 4) else None]
                if gi == 0:
                    for fc2_ in (h, h + 8):
                        if fc2_ < NFC // 2:
                            prefetch_wfi(fc2_)
                nqb = T // 256
                for qb in range(nqb):
                    ktiles = list(range(nkt_all)) if rope else [2 * qb, 2 * qb + 1]
                    nk_ = len(ktiles)
                    qs = slice(qb * 256, (qb + 1) * 256)
                    es_ = qbi[0] % 2
                    qbi[0] += 1
                    abase = 3
                    ACC = ["p%d" % abase, "p%d" % (abase + 1)]

                    def emit_av(ki, kt, hb_=hb_, vn=vn, nk_=nk_, abase=abase, ACC=ACC):
                        pb = ki % 4
                        first, last = (ki == 0), (ki == nk_ - 1)
                        fns = []
                        for qt in range(2):
                            for m in range(2):
                                bank = abase + qt
                                st_ = first and (m == 0)
                                fns.append(lambda e, pb=pb, kt=kt, qt=qt, m=m, bank=bank, st_=st_, last=last, hb_=hb_: e.matmul(
                                    psum[:, bank, m * 256:m * 256 + 129], lhsT=pT[pb][:, m * 256 + qt * 128:m * 256 + (qt + 1) * 128], rhs=vh[hb_][:, kt, :], start=st_, stop=last, skip_group_check=True))
                        peg(fns, r=["pT%d" % pb, vn], w=ACC)

                    for ki, kt in enumerate(ktiles):
                        sb_ = ki % 3
                        pb = ki % 4
                        peg([(lambda e, m=m, kt=kt, sb_=sb_, qs=qs, hb_=hb_: e.matmul(psum[:, sb_, m * 256:(m + 1) * 256], lhsT=kT[hb_][:, kt * 128:(kt + 1) * 128], rhs=qTm[hb_][m][:, qs], start=True, stop=True)) for m in range(2)],
                            r=[qn, kn], w=["p%d" % sb_])
                        ev(ACT, lambda e, sb_=sb_, pb=pb: e.activation(out=pT[pb], in_=psum[:, sb_, :], func=AF.Exp, scale=SCALE), w=["p%d" % sb_, "pT%d" % pb])
                        if ki >= 2:
                            emit_av(ki - 2, ktiles[ki - 2])
                        gstep[0] += 1
                        run_due()
                        astep[0] += 1
                        if rope:
                            if pending_parts and astep[0] >= 3 and astep[0] % 2 == 0:
                                pending_parts.pop(0)()
                        else:
                            for _ in range(5):
                                if pending_parts:
                                    pending_parts.pop(0)()
                            if mr_j[0] is not None:
                                mr_state = start_modrows(mr_j[0], use_ring=False, extra=bt)
                                mr_j[0] = None
                    for ki in range(max(0, nk_ - 2), nk_):
                        emit_av(ki, ktiles[ki])
                    a0, a1, a2 = att[es_]
                    an = "att%d" % es_
                    sc = att_s[:, es_, :]
                    scn = "atts%d" % es_
                    accv = psum[:, abase:abase + 2, :].rearrange("p q (m c) -> p q m c", m=2)
                    araw = araws[es_]
                    arn = "araw%d" % es_
                    ev(DVE, lambda e, accv=accv, araw=araw: e.tensor_copy(out=araw, in_=accv[:, :, :, 0:129]), w=ACC + [arn])
                    ev(DVE, lambda e, araw=araw, sc=sc: e.reciprocal(out=sc[:, 0:4].rearrange("p (q m) -> p q m", m=2), in_=araw[:, :, :, 128]), r=[arn], w=[scn])
                    rzv = sc[:, 0:4].rearrange("p (q m) -> p q m", m=2)
                    ev(DVE, lambda e, rzv=rzv: e.tensor_scalar(out=rzv[:, :, 1], in0=rzv[:, :, 1], scalar1=neglam, scalar2=None, op0=ALU.mult), r=["lamt"], w=[scn])
                    O = araw[:, :, :, 0:128]
                    ev(DVE, lambda e, O=O, rzv=rzv, a0=a0: e.tensor_tensor(out=a0, in0=O[:, :, 1, :], in1=bass.AP(rzv.tensor, rzv.offset + 1, [list(rzv.ap[0]), [2, 2], [0, 128]]), op=ALU.mult), r=[scn, arn], w=[an + "a"])
                    ev(DVE, lambda e, O=O, rzv=rzv, a1=a1: e.tensor_tensor(out=a1, in0=O[:, :, 0, :], in1=bass.AP(rzv.tensor, rzv.offset, [list(rzv.ap[0]), [2, 2], [0, 128]]), op=ALU.mult), r=[scn, arn], w=[an + "b"])
                    ev(DVE, lambda e, a0=a0, a1=a1: e.tensor_tensor(out=a1, in0=a1, in1=a0, op=ALU.add), r=[an + "a"], w=[an + "b"])
                    ev(DVE, lambda e, a1=a1, a2=a2: e.tensor_tensor(out=a2, in0=a1, in1=a1, op=ALU.mult), r=[an + "b"], w=[an + "c"])
                    ev(DVE, lambda e, a2=a2, sc=sc: e.tensor_reduce(out=sc[:, 4:6], in_=a2, axis=AX.X, op=ALU.add), r=[an + "c"], w=[scn])

                    def stage1(sc=sc, scn=scn, a1=a1, an=an, es_=es_):
                        ev(ACT, lambda e: e.activation(out=sc[:, 6:8], in_=sc[:, 4:6], func=AF.Ln, scale=1.0 / 128, bias=epsb), r=["epsb"], w=[scn])
                        ev(ACT, lambda e: e.activation(out=sc[:, 6:8], in_=sc[:, 6:8], func=AF.Exp, scale=-0.5), w=[scn])
                        rsv = sc[:, 6:8]
                        ev(DVE, lambda e: e.tensor_tensor(out=a1, in0=a1, in1=bass.AP(rsv.tensor, rsv.offset, [list(rsv.ap[0]), [1, 2], [0, 128]]), op=ALU.mult), r=[scn], w=[an + "b"])
                        ev(DVE, lambda e: e.tensor_tensor(out=ost[es_], in0=a1, in1=bass.AP(swb.tensor, swb.offset, [list(swb.ap[0]), [0, 2], [1, 128]]), op=ALU.mult), r=[an + "b", "swb"], w=["ost%d" % es_])

                    def stage2(es_=es_):
                        psb = psum[:, 7, :].bitcast(BF16)
                        peg([(lambda e, qt=qt: e.transpose(out=psb[:, qt * 128:(qt + 1) * 128], in_=ost[es_][:, qt, :], identity=ident)) for qt in range(2)], r=["ost%d" % es_, "ident"], w=["p7"])

                    def stage3(h=h, qs=qs):
                        psb = psum[:, 7, :].bitcast(BF16)
                        ev(ACT, lambda e: e.activation(out=oT[:, h, qs], in_=psb[:, 0:256], func=AF.Copy), w=["p7", "oT"])

                    for dly, fn_ in (((8, stage1), (11, stage2), (12, stage3)) if rope else ((4, stage1), (7, stage2), (8, stage3))):
                        dq.append((gstep[0] + dly, dseq[0], fn_))
                        dseq[0] += 1
                    dq.sort(key=lambda t: (t[0], t[1]))
                if mr_state is not None:
                    finish_modrows(mr_state)
            run_due(force=True)
            if not rope:
                for s_ in (PE, ACT, DVE):
                    s_.pending.extend(dep.get([], ["kvo0", "kvo1", "mr_row0", "mr_row1", "mr_bb", "mr_nw", "mslab0", "mslab1"]))
            barrier()
            _chk('B%d' % gi)

            W = S + 16
            upads = [view(RA, 0, [nseq, W], F32), view(RA, 48 * KB, [nseq, W], F32)]
            sA = view(RA, 9 * KB, [nseq, W], F32)
            sB = view(RA, 18 * KB, [nseq, W], F32)
            pooled = [view(RA, 27 * KB, [T], BF16), view(RA, 60 * KB, [T], BF16)]
            mixedT = view(RA, 32 * KB, [4, T], BF16)
            sg = [[view(RA, 48 * KB + (2 * s_ + j) * 2 * KB, [512], F32) for j in range(2)] for s_ in range(2)]
            ev(DVE, lambda e: e.memset(upads[0], 0.0), w=["upad0"])
            ev(DVE, lambda e: e.memset(upads[1], 0.0), w=["upad1"])
            ev(DVE, lambda e: e.memset(sA, 0.0), w=["sA"])
            ev(DVE, lambda e: e.memset(sB, 0.0), w=["sB"])
            etmp = view(FS, 0, [64], F32)
            sv, rn = ring_load([(lambda s: slab3(s, 0, 8, 512), wsrc(w_in, 0, 512))])
            wu = slab3(sv, 0, 8, 512)
            spb = 512 // S if S < 512 else 1
            def c1_proj(g):
                up = upads[g % 2]
                un = "upad%d" % (g % 2)
                for tb in range(ntb):
                    bank = tb % 2
                    peg([(lambda e, k=k, g=g, tb=tb, bank=bank: e.matmul(psum[:, bank, :], lhsT=wu[:, k, g * 128:(g + 1) * 128], rhs=hT[:, k, tb * 512:(tb + 1) * 512], start=(k == 0), stop=(k == 7))) for k in range(8)],
                        r=[rn, "hT"], w=["p%d" % bank])
                    if S >= 512:
                        dstu = up[:, 0, 8 + tb * 512:8 + (tb + 1) * 512]
                        srcu = psum[:, bank, :]
                    else:
                        dstu = up[:, tb * spb:(tb + 1) * spb, 8:8 + S]
                        srcu = psum[:, bank, :].rearrange("p (s t) -> p s t", s=spb)
                    ev(ACT, lambda e, dstu=dstu, srcu=srcu: e.activation(out=dstu, in_=srcu, func=AF.Copy), w=["p%d" % bank, un])

            def c1_pool(g):
                up = upads[g % 2]
                un = "upad%d" % (g % 2)
                pl = pooled[g % 2]
                pn = "pooled%d" % (g % 2)
                w = 2 << g
                ev(DVE, lambda e: e.tensor_tensor(out=sA[:, :, 1:W], in0=up[:, :, 0:W - 1], in1=up[:, :, 1:W], op=ALU.add), r=[un], w=["sA"])
                cur = sA; cn = "sA"
                if w >= 4:
                    ev(DVE, lambda e: e.tensor_tensor(out=sB[:, :, 2:W - 1], in0=sA[:, :, 1:W - 2], in1=sA[:, :, 3:W], op=ALU.add), r=["sA"], w=["sB"])
                    cur = sB; cn = "sB"
                if w >= 8:
                    ev(DVE, lambda e: e.tensor_tensor(out=sA[:, :, 4:W - 3], in0=sB[:, :, 2:W - 5], in1=sB[:, :, 6:W - 1], op=ALU.add), r=["sB"], w=["sA"])
                    cur = sA; cn = "sA"
                if w >= 16:
                    ev(DVE, lambda e: e.tensor_tensor(out=sB[:, :, 8:W - 7], in0=sA[:, :, 4:W - 11], in1=sA[:, :, 12:W - 3], op=ALU.add), r=["sA"], w=["sB"])
                    cur = sB; cn = "sB"
                pv = pl.rearrange("p (s t) -> p s t", s=nseq)
                ev(DVE, lambda e, cur=cur, w=w, pv=pv: e.scalar_tensor_tensor(out=pv, in0=cur[:, :, 8:8 + S], scalar=1.0 / w, in1=up[:, :, 8:8 + S], op0=ALU.mult, op1=ALU.subtract), r=[cn, un], w=[pn])
                hw_ = w // 2
                eL = bass.AP(ect.tensor, ect.offset + g * 16, [list(ect.ap[0]), [0, nseq], [1, hw_]])
                ev(DVE, lambda e, cur=cur, eL=eL, hw_=hw_: e.tensor_tensor(out=etmp[:, 0:nseq * hw_].rearrange("p (s t) -> p s t", s=nseq), in0=cur[:, :, 8:8 + hw_], in1=eL, op=ALU.mult), r=[cn, "ect"], w=["etmp"])
                ev(DVE, lambda e, pv=pv, hw_=hw_: e.tensor_tensor(out=pv[:, :, 0:hw_], in0=etmp[:, 0:nseq * hw_].rearrange("p (s t) -> p s t", s=nseq), in1=up[:, :, 8:8 + hw_], op=ALU.subtract), r=["etmp", un], w=[pn])
                if hw_ > 1:
                    nr = hw_ - 1
                    eR = bass.AP(ect.tensor, ect.offset + g * 16 + 8, [list(ect.ap[0]), [0, nseq], [1, nr]])
                    r0 = S - hw_ + 1
                    ev(DVE, lambda e, cur=cur, eR=eR, nr=nr, r0=r0: e.tensor_tensor(out=etmp[:, 0:nseq * nr].rearrange("p (s t) -> p s t", s=nseq), in0=cur[:, :, 8 + r0:8 + r0 + nr], in1=eR, op=ALU.mult), r=[cn, "ect"], w=["etmp"])
                    ev(DVE, lambda e, pv=pv, nr=nr, r0=r0: e.tensor_tensor(out=pv[:, :, r0:r0 + nr], in0=etmp[:, 0:nseq * nr].rearrange("p (s t) -> p s t", s=nseq), in1=up[:, :, 8 + r0:8 + r0 + nr], op=ALU.subtract), r=["etmp", un], w=[pn])
                for tb in range(ntb):
                    bank = 2 + tb % 2
                    peg([lambda e, g=g, tb=tb, bank=bank, pl=pl: e.matmul(psum[:, bank, :], lhsT=wpool[:, g, :], rhs=pl[:, tb * 512:(tb + 1) * 512], start=True, stop=True)],
                        r=[pn, "wpool"], w=["p%d" % bank])
                    ev(ACT, lambda e, g=g, tb=tb, bank=bank: e.activation(out=mixedT[:, g, tb * 512:(tb + 1) * 512], in_=psum[:, bank, :], func=AF.Copy, scale=pscale[:, g:g + 1]), r=["pscale"], w=["p%d" % bank, "mixedT"])

            c1_proj(0)
            for g in range(4):
                if g + 1 < 4:
                    c1_proj(g + 1)
                c1_pool(g)
            barrier()

            it = 0
            for mo in range(8):
                sv, rn = ring_load([
                    (lambda s: slab3(s, 0, 8, 256)[:, :, 0:128], wsrc(w_in, 3584 + mo * 128, 128)),
                    (lambda s: slab3(s, 0, 8, 256)[:, :, 128:256], wsrc(w_in, 3584 + 1024 + mo * 128, 128)),
                    (lambda s: slab3(s, 2048, 4, 128), wsrc(w_proj_a, mo * 128, 128, 0, 4)),
                    (lambda s: slab3(s, 2560, 8, 128), wsrc(w_proj_b, mo * 128, 128)),
                ])
                wg = slab3(sv, 0, 8, 256)
                wa = slab3(sv, 2048, 4, 128)
                wb = slab3(sv, 2560, 8, 128)
                for tb in range(ntb):
                    s_ = it % 2
                    b0 = 4 * s_
                    it += 1
                    cs = slice(tb * 512, (tb + 1) * 512)
                    for j in range(2):
                        peg([(lambda e, k=k, j=j, b0=b0, cs=cs, wg=wg: e.matmul(psum[:, b0 + j, :], lhsT=wg[:, k, j * 128:(j + 1) * 128], rhs=hT[:, k, cs], start=(k == 0), stop=(k == 7))) for k in range(8)],
                            r=[rn, "hT"], w=["p%d" % (b0 + j)])
                    peg([(lambda e, k=k, b0=b0, cs=cs, wa=wa: e.matmul(psum[:, b0 + 2, :], lhsT=wa[:, k, :], rhs=mixedT[:, k, cs], start=(k == 0), stop=(k == 3))) for k in range(4)],
                        r=[rn, "mixedT"], w=["p%d" % (b0 + 2)])
                    peg([(lambda e, k=k, b0=b0, cs=cs, wb=wb: e.matmul(psum[:, b0 + 3, :], lhsT=wb[:, k, :], rhs=oT[:, k, cs], start=(k == 0), stop=(k == 7))) for k in range(8)],
                        r=[rn, "oT"], w=["p%d" % (b0 + 3)])
                    g0, g1 = sg[s_]
                    ev(ACT, lambda e, b0=b0, g0=g0: e.activation(out=g0, in_=psum[:, b0, :], func=AF.Sigmoid), w=["p%d" % b0, "sg%d_0" % s_])
                    ev(ACT, lambda e, b0=b0, g1=g1: e.activation(out=g1, in_=psum[:, b0 + 1, :], func=AF.Sigmoid), w=["p%d" % (b0 + 1), "sg%d_1" % s_])
                    ev(DVE, lambda e, b0=b0, g0=g0: e.tensor_tensor(out=g0, in0=psum[:, b0 + 2, :], in1=g0, op=ALU.mult), w=["p%d" % (b0 + 2), "sg%d_0" % s_])
                    ev(DVE, lambda e, b0=b0, g1=g1: e.tensor_tensor(out=g1, in0=psum[:, b0 + 3, :], in1=g1, op=ALU.mult), w=["p%d" % (b0 + 3), "sg%d_1" % s_])
                    ev(DVE, lambda e, mo=mo, cs=cs, g0=g0, g1=g1: e.tensor_tensor(out=mergedT[:, mo, cs], in0=g0, in1=g1, op=ALU.add), r=["sg%d_0" % s_, "sg%d_1" % s_], w=["mergedT"])
            barrier()
            _chk('C%d' % gi)

            load_mod(cond, 2, 0)
            for s_ in (PE, ACT, DVE, SP):
                s_.pending.extend(dep.get([], ["xo"]))
            xh = [view(FS, b * 2 * KB, [512], F32) for b in range(2)]
            dt_ = [view(FS, 4 * KB + b * 2 * KB, [512], F32) for b in range(4)]
            wfo = view(RC, 0, [NFC, 1024], BF16)
            btD = [PE.last, DVE.last, ACT.last]
            slabD = [ring_load([(lambda s, hf=hf: slab3(s, 0, 8, 512), wsrc(w_out, hf * 512, 512))]) for hf in range(2)]
            dm(POOL, [(lambda e, k=k: e.dma_start(out=wfo[:, k, :], in_=w_ffn_out.ap()[k * 128:(k + 1) * 128, :])) for k in range(16)], dsem["wfo"], w=["wfoA"], extra=btD)
            it = 0
            for hf in range(2):
                sv, rn = slabD[hf]
                wo = slab3(sv, 0, 8, 512)
                hs = slice(hf * 512, (hf + 1) * 512)
                for i in range(nt):
                    b = it % 2
                    b4 = it % 4
                    it += 1
                    dm(SP, [lambda e, i=i, b=b, hs=hs: e.dma_start(out=xh[b], in_=x_dram.ap()[i * 128:(i + 1) * 128, hs])], dsem["xl%d" % b], w=["xh%d" % b])
                    peg([(lambda e, k=k, i=i, b4=b4, wo=wo: e.matmul(psum[:, 2 + b4, :], lhsT=mergedT[:, k, i * 128:(i + 1) * 128], rhs=wo[:, k, :], start=(k == 0), stop=(k == 7))) for k in range(8)],
                        r=[rn, "mergedT"], w=["p%d" % (2 + b4)])
                    ev(DVE, lambda e, b4=b4, hs=hs: e.tensor_tensor(out=dt_[b4], in0=psum[:, 2 + b4, :], in1=modv[0][:, hs], op=ALU.mult), r=["mod0"], w=["p%d" % (2 + b4), "dt%d" % b4])
                    ev(DVE, lambda e, b=b, b4=b4, hs=hs, i=i: e.tensor_tensor(out=x1[:, i, hs], in0=dt_[b4], in1=xh[b], op=ALU.add), r=["dt%d" % b4, "xh%d" % b], w=["x1"])
            barrier()
            _chk('D%d' % gi)

            load_mod(cond, 3, 0)
            load_mod(cond, 4, 1)
            h2T = hT
            btE = [PE.last, DVE.last, ACT.last]
            dm(POOL, [(lambda e, k=k: e.dma_start(out=wfo[:, k, :], in_=w_ffn_out.ap()[k * 128:(k + 1) * 128, :])) for k in range(16, NFC)], dsem["wfo2"], w=["wfoB"], extra=btE)
            stageE = ([view(FS, 0, [1024], F32), view(FS, 4 * KB, [1024], F32)],
                      [view(FS, 8 * KB, [1024], BF16), view(FS, 10 * KB, [1024], BF16)],
                      view(RC, 64 * KB, [1024], BF16))
            norm_phase(T, None, x1, 0, 1, h2T, RC, 0, "hT", stage=stageE)
            barrier()
            _chk('E%d' % gi)

            load_mod(cond, 5, 0)
            bt = [PE.last, DVE.last, ACT.last]
            dm(SP, [lambda e: e.dma_start(out=modv[1], in_=bass.AP(fnw, 0, [[0, 128], [1, 1024]]))], misc, w=["mod1"], extra=bt)
            gT = view(RC, 44 * KB, [NFC, 512], BF16)
            c0s = [view(FS, s_ * 4 * KB, [512], F32) for s_ in range(2)]
            c1s = [view(FS, s_ * 4 * KB + 2 * KB, [512], F32) for s_ in range(2)]
            xo = view(FS, 8 * KB, [1024], F32)
            itf = 0
            f1_pend = [None]
            for tb in range(ntb):
                t0 = tb * 512
                for fc2 in range(NFC // 2):
                    sv, rn = ring_load([(lambda s: s, wfi_scr.ap()[fc2])], r=["wsc%d" % fc2])
                    wf = slab3(sv, 0, 8, 512)
                    for f2 in range(2):
                        fc = fc2 * 2 + f2
                        s_ = itf % 2
                        itf += 1
                        ba, bu = 2 * s_, 2 * s_ + 1
                        c0, c1 = c0s[s_], c1s[s_]
                        c0n, c1n = "c0_%d" % s_, "c1_%d" % s_
                        peg([(lambda e, k=k, f2=f2, ba=ba, t0=t0, wf=wf: e.matmul(psum[:, ba, :], lhsT=wf[:, k, f2 * 128:(f2 + 1) * 128], rhs=h2T[:, k, t0:t0 + 512], start=(k == 0), stop=(k == 7))) for k in range(8)],
                            r=[rn, "hT"], w=["p%d" % ba])
                        peg([(lambda e, k=k, f2=f2, bu=bu, t0=t0, wf=wf: e.matmul(psum[:, bu, :], lhsT=wf[:, k, 256 + f2 * 128:256 + (f2 + 1) * 128], rhs=h2T[:, k, t0:t0 + 512], start=(k == 0), stop=(k == 7))) for k in range(8)],
                            r=[rn, "hT"], w=["p%d" % bu])
                        need_l = (t0 % S) != 0
                        need_r = ((t0 + 512) % S) != 0
                        cols = ([t0 - 1] if need_l else []) + ([t0 + 512] if need_r else [])
                        if cols:
                            fns = []
                            for ci, col in enumerate(cols):
                                for k in range(8):
                                    fns.append(lambda e, k=k, f2=f2, col=col, ci=ci, s_=s_, wf=wf: e.matmul(psum[:, 4, 2 * s_ + ci:2 * s_ + ci + 1], lhsT=wf[:, k, f2 * 128:(f2 + 1) * 128], rhs=h2T[:, k, col:col + 1], start=(k == 0), stop=(k == 7)))
                            peg(fns, r=[rn, "hT"], w=["p4"])
                        ev(ACT, lambda e, fc=fc, ba=ba, c0=c0: e.activation(out=c0, in_=psum[:, ba, :], func=AF.Identity, scale=cw[:, 1, fc:fc + 1], bias=cb[:, fc:fc + 1]), r=["cw", "cb"], w=["p%d" % ba, c0n])
                        nsq = 512 // S if S < 512 else 1
                        Sq = 512 // nsq
                        Av = psum[:, ba, :].rearrange("p (s t) -> p s t", s=nsq)
                        c0v = c0.rearrange("p (s t) -> p s t", s=nsq)
                        ev(DVE, lambda e, fc=fc, Av=Av, c0v=c0v, Sq=Sq: e.scalar_tensor_tensor(out=c0v[:, :, 1:Sq], in0=Av[:, :, 0:Sq - 1], scalar=cw[:, 0, fc:fc + 1], in1=c0v[:, :, 1:Sq], op0=ALU.mult, op1=ALU.add), r=["cw"], w=["p%d" % ba, c0n])
                        ev(DVE, lambda e, fc=fc, Av=Av, c0v=c0v, Sq=Sq: e.scalar_tensor_tensor(out=c0v[:, :, 0:Sq - 1], in0=Av[:, :, 1:Sq], scalar=cw[:, 2, fc:fc + 1], in1=c0v[:, :, 0:Sq - 1], op0=ALU.mult, op1=ALU.add), r=["cw"], w=["p%d" % ba, c0n])
                        if need_l:
                            ev(DVE, lambda e, fc=fc, c0=c0, s_=s_: e.scalar_tensor_tensor(out=c0[:, 0:1], in0=psum[:, 4, 2 * s_:2 * s_ + 1], scalar=cw[:, 0, fc:fc + 1], in1=c0[:, 0:1], op0=ALU.mult, op1=ALU.add), r=["cw"], w=["p4", c0n])
                        if need_r:
                            ci = 1 if need_l else 0
                            ev(DVE, lambda e, fc=fc, c0=c0, s_=s_, ci=ci: e.scalar_tensor_tensor(out=c0[:, 511:512], in0=psum[:, 4, 2 * s_ + ci:2 * s_ + ci + 1], scalar=cw[:, 2, fc:fc + 1], in1=c0[:, 511:512], op0=ALU.mult, op1=ALU.add), r=["cw"], w=["p4", c0n])
                        def f1_tail(fc=fc, bu=bu, c0=c0, c1=c1, c0n=c0n, c1n=c1n):
                            ev(ACT, lambda e: e.activation(out=c1, in_=c0, func=AF.Silu), r=[c0n], w=[c1n])
                            ev(DVE, lambda e: e.tensor_tensor(out=gT[:, fc, :], in0=psum[:, bu, :], in1=c1, op=ALU.mult), r=[c1n], w=["p%d" % bu, "gT"])
                        if f1_pend[0] is not None:
                            f1_pend[0]()
                        f1_pend[0] = f1_tail
                if f1_pend[0] is not None:
                    f1_pend[0]()
                    f1_pend[0] = None
                for tt in range(4):
                    i = tb * 4 + tt
                    for hf in range(2):
                        bank = 5 + hf
                        peg([(lambda e, k=k, tt=tt, hf=hf, bank=bank: e.matmul(psum[:, bank, :], lhsT=gT[:, k, tt * 128:(tt + 1) * 128], rhs=wfo[:, k, hf * 512:(hf + 1) * 512], start=(k == 0), stop=(k == NFC - 1))) for k in range(NFC)],
                            r=["gT", "wfoA", "wfoB"], w=["p%d" % bank])
                        hs = slice(hf * 512, (hf + 1) * 512)
                        ev(DVE, lambda e, bank=bank, hs=hs: e.tensor_tensor(out=psum[:, bank, :], in0=psum[:, bank, :], in1=modv[0][:, hs], op=ALU.mult), r=["mod0"], w=["p%d" % bank])
                        ev(DVE, lambda e, bank=bank, hs=hs, i=i: e.tensor_tensor(out=x1[:, i, hs], in0=psum[:, bank, :], in1=x1[:, i, hs], op=ALU.add), w=["p%d" % bank, "x1"])
                    sl_ = slice(i % 16, i % 16 + 1)
                    junk2 = c0s[0].bitcast(BF16)
                    ev(ACT, lambda e, sl_=sl_, junk2=junk2, i=i: e.activation(out=junk2, in_=x1[:, i, :], func=AF.Square, accum_out=ss_t[:, sl_]), r=["x1"], w=["c0_0", "ss%d" % (i % 16)])
                    ev(ACT, lambda e, sl_=sl_: e.activation(out=rs_t[:, sl_], in_=ss_t[:, sl_], func=AF.Sqrt, scale=1.0 / D, bias=EPS), r=["ss%d" % (i % 16)], w=["rs%d" % (i % 16)])
                    ev(DVE, lambda e, sl_=sl_: e.reciprocal(out=rs_t[:, sl_], in_=rs_t[:, sl_]), w=["rs%d" % (i % 16)])
                    ev(DVE, lambda e, sl_=sl_, i=i: e.scalar_tensor_tensor(out=xo, in0=x1[:, i, :], scalar=rs_t[:, sl_], in1=modv[1], op0=ALU.mult, op1=ALU.mult), r=["x1", "rs%d" % (i % 16), "mod1"], w=["xo"])
                    dm(SP, [lambda e, i=i: e.dma_start(out=y_dram.ap()[i * 128:(i + 1) * 128, :], in_=xo)], dsem["st2"], r=["xo"])
            barrier()

        try:
            _chk('K')
            run_group(0, 1024, 4, 256, 0, False, xp, yp)
            _chk('F0')
            run_group(1, 2048, 1, 2048, 256, True, xs, ys)
        except _Stop:
            pass

        fin = [(dsem[n][0], dsem[n][1]) for n in ["st0", "st1", "st2", "st3", "st4"] if dsem[n][1] > 0]
        SP.wait_only(fin)
        for s_ in (PE, ACT, DVE, POOL):
            s_.wait_only(fin)

        @block.tensor
        def _(e):
            PE.emit(e)

        @block.scalar
        def _(e):
            ACT.emit(e)

        @block.vector
        def _(e):
            DVE.emit(e)

        @block.gpsimd
        def _(e):
            POOL.emit(e)

        @block.sync
        def _(e):
            SP.emit(e)
    return nc

def _consts():
    ident = np.eye(128, dtype=np.float32)
    t = np.arange(2048)
    row = (t // 64).astype(np.float32)
    col = (t % 64).astype(np.float32)
    inv = (np.float32(10000.0) ** (-np.arange(16, dtype=np.float32) / np.float32(16))).astype(np.float32)
    cosT = np.zeros((128, 2048), np.float32)
    sinT = np.zeros((128, 2048), np.float32)
    for p in range(128):
        j = p % 64
        a = j // 32
        f = j % 16
        ang = ((row if a == 0 else col) * inv[f]).astype(np.float32)
        cosT[p] = np.cos(ang)
        sinT[p] = np.sin(ang)
    ect = np.ones((4, 16), np.float32)
    for g in range(4):
        w = 2 << g
        hw = w // 2
        for j in range(hw):
            ect[g, j] = 1.0 / (j + hw)
        for j in range(hw - 1):
            ect[g, 8 + j] = 1.0 / (w - 1 - j)
    ect = np.broadcast_to(ect.reshape(1, 64), (128, 64)).copy()
    pm = np.zeros((128, 128), np.float32)
    for m in range(128):
        if (m % 32) < 16:
            pm[m + 16, m] = -1.0
        else:
            pm[m - 16, m] = 1.0
    return ident, cosT, sinT, ect, pm


_NC_CACHE = {}


def kernel(x_prompt, x_sample, cache_k, cache_v, c, c_ctx, w_ada, b_ada, w_in, w_pool, pool_scale,
           w_proj_a, w_proj_b, w_out, lam_q1, lam_k1, lam_q2, lam_k2, subln_w, norm1_w, norm2_w,
           w_ffn_in, ffn_conv_w, ffn_conv_b, w_ffn_out, final_norm_w):
    f = lambda a: np.ascontiguousarray(np.asarray(a, dtype=np.float32))
    if "nc" not in _NC_CACHE:
        _NC_CACHE["nc"] = build_program()
    nc = _NC_CACHE["nc"]
    ident, cosT, sinT, ect, pm = _consts()
    x_prompt = f(x_prompt); x_sample = f(x_sample); cache_k = f(cache_k); cache_v = f(cache_v); c = f(c); c_ctx = f(c_ctx)
    shared = {
        "w_ada": f(w_ada)[0], "b_ada": f(b_ada)[0], "w_in": f(w_in)[0], "w_pool": f(w_pool)[0], "pool_scale": f(pool_scale)[0],
        "w_proj_a": f(w_proj_a)[0], "w_proj_b": f(w_proj_b)[0], "w_out": f(w_out)[0],
        "lamv": np.ascontiguousarray(np.stack([f(lam_q1)[0], f(lam_k1)[0], f(lam_q2)[0], f(lam_k2)[0]], 0)),
        "subln_w": f(subln_w)[0], "norm1_w": f(norm1_w)[0], "norm2_w": f(norm2_w)[0],
        "w_ffn_in": f(w_ffn_in)[0], "conv_w": f(ffn_conv_w)[0], "conv_b": f(ffn_conv_b)[0], "w_ffn_out": f(w_ffn_out)[0],
        "fnw": f(final_norm_w), "c_ident": ident, "c_cos": cosT, "c_sin": sinT, "c_ect": ect, "c_pm": pm,
    }
    in_maps = []
    for i in range(N_CORES):
        m = dict(shared)
        m["xp"] = np.ascontiguousarray(x_prompt[4 * i:4 * i + 4].reshape(1024, D))
        m["xs"] = np.ascontiguousarray(x_sample[i].reshape(2048, D))
        m["ck"] = np.ascontiguousarray(cache_k[i, 0].reshape(256, D))
        m["cv"] = np.ascontiguousarray(cache_v[i, 0].reshape(256, D))
        m["cc"] = np.ascontiguousarray(np.stack([c_ctx, c[i]], 0))
        in_maps.append(m)
    res = run_bass_kernel_spmd(nc, in_maps, core_ids=list(range(N_CORES)))
    r = res.results
    y_prompt = np.concatenate([r[i]["yp"].reshape(4, 256, D) for i in range(N_CORES)], 0)
    y_sample = np.stack([r[i]["ys"].reshape(2048, D) for i in range(N_CORES)], 0)
    new_k = np.concatenate([r[i]["nk"].reshape(4, 1, 256, NH, 2, 64) for i in range(N_CORES)], 0)
    new_v = np.concatenate([r[i]["nv"].reshape(4, 1, 256, NH, 128) for i in range(N_CORES)], 0)
    return (y_prompt.astype(np.float32), y_sample.astype(np.float32), new_k.astype(np.float32), new_v.astype(np.float32))
```

```python
import contextlib
import os


class _Stop(Exception):
    pass


def _chk(tag):
    if os.environ.get('MK_STOP') == tag:
        raise _Stop()

import numpy as np
import concourse.bass as bass
import concourse.mybir as mybir
from concourse.bass_utils import run_bass_kernel_spmd

F32 = mybir.dt.float32
BF16 = mybir.dt.bfloat16
AF = mybir.ActivationFunctionType
ALU = mybir.AluOpType
AX = mybir.AxisListType

D = 1024
NH = 8
DFF = 2816
NFC = 22
DIN = 5632
EPS = 1e-6
LAM_INIT = 0.2
N_CORES = 8


class Stream:
    def __init__(self, name, sem, skip_self):
        self.name = name
        self.ops = []
        self.sem = sem
        self.count = 0
        self.waited = {}
        self.pending = []
        self.last = None
        self.skip_self = skip_self

    def _waits(self, waits):
        w = []
        allw = list(waits) + self.pending
        self.pending = []
        for tok in allw:
            if tok is None:
                continue
            toks = tok if isinstance(tok, list) else [tok]
            for t in toks:
                if t is None:
                    continue
                s, v = t
                if s is self.sem and self.skip_self:
                    continue
                if self.waited.get(id(s), 0) >= v:
                    continue
                self.waited[id(s)] = v
                w.append((s, v))
        return w

    def op(self, fn, waits=(), signal=True):
        w = self._waits(waits)
        tok = None
        if signal:
            self.count += 1
            tok = (self.sem, self.count)
            self.last = tok
        self.ops.append((w, fn, (self.sem, 1) if signal else None))
        return tok

    def dma(self, fn, semc, waits=()):
        w = self._waits(waits)
        semc[1] += 16
        self.ops.append((w, fn, (semc[0], 16)))
        return (semc[0], semc[1])

    def wait_only(self, waits):
        w = self._waits(waits)
        self.ops.append((w, None, None))

    def emit(self, eng):
        for w, fn, inc in self.ops:
            for s, v in w:
                eng.wait_ge(s, v)
            if fn is None:
                continue
            ins = fn(eng)
            if inc is not None:
                ins.then_inc(inc[0], inc[1])


class Dep:
    def __init__(self):
        self.lw = {}
        self.rd = {}

    def get(self, r, w):
        out = []
        for n in r:
            if n in self.lw:
                out.append(self.lw[n])
        for n in w:
            if n in self.lw:
                out.append(self.lw[n])
            out.extend(self.rd.get(n, {}).values())
        return out

    def commit(self, tok, r, w):
        for n in r:
            d = self.rd.setdefault(n, {})
            k = id(tok[0])
            if k not in d or d[k][1] < tok[1]:
                d[k] = tok
        for n in w:
            self.lw[n] = tok
            self.rd[n] = {}


def build_program():
    nc = bass.Bass("TRN2", target_bir_lowering=False)
    dt_in = lambda name, shape: nc.dram_tensor(name, list(shape), F32, kind="ExternalInput")
    dt_out = lambda name, shape: nc.dram_tensor(name, list(shape), F32, kind="ExternalOutput")
    xp = dt_in("xp", [1024, D])
    xs = dt_in("xs", [2048, D])
    ck = dt_in("ck", [256, D])
    cv = dt_in("cv", [256, D])
    cc = dt_in("cc", [2, D])
    w_ada = dt_in("w_ada", [D, 6 * D])
    b_ada = dt_in("b_ada", [6 * D])
    w_in = dt_in("w_in", [D, DIN])
    w_pool = dt_in("w_pool", [4, 128, 128])
    pool_scale = dt_in("pool_scale", [512])
    w_proj_a = dt_in("w_proj_a", [512, D])
    w_proj_b = dt_in("w_proj_b", [D, D])
    w_out = dt_in("w_out", [D, D])
    lamv = dt_in("lamv", [4, 64])
    subln_w = dt_in("subln_w", [128])
    norm1_w = dt_in("norm1_w", [D])
    norm2_w = dt_in("norm2_w", [D])
    w_ffn_in = dt_in("w_ffn_in", [D, 2 * DFF])
    conv_w = dt_in("conv_w", [3, DFF])
    conv_b = dt_in("conv_b", [DFF])
    w_ffn_out = dt_in("w_ffn_out", [DFF, D])
    fnw = dt_in("fnw", [D])
    c_ident = dt_in("c_ident", [128, 128])
    c_cos = dt_in("c_cos", [128, 2048])
    c_sin = dt_in("c_sin", [128, 2048])
    c_ect = dt_in("c_ect", [128, 64])
    c_pm = dt_in("c_pm", [128, 128])
    yp = dt_out("yp", [1024, D])
    ys = dt_out("ys", [2048, D])
    nk = dt_out("nk", [1024, D])
    nv = dt_out("nv", [1024, D])

    es = contextlib.ExitStack()
    with es:
        def sbt(name, nfree, dt=F32):
            return es.enter_context(nc.sbuf_tensor(name, [128, nfree], dt))

        KB = 1024
        RA = sbt("RA", 64 * KB // 4)
        RB = sbt("RB", 32 * KB // 4)
        RC = sbt("RC", 66 * KB // 4)
        RING = sbt("RING", 16 * KB // 4)
        MOD = sbt("MOD", 8 * KB // 4)
        CST = sbt("CST", 9 * KB // 4)
        FS = sbt("FS", 12 * KB // 4)
        psum = es.enter_context(nc.psum_tensor("psum", [128, 8, 512], F32))

        def view(reg, off, shape, dt):
            esz = 2 if dt == BF16 else 4
            n = int(np.prod(shape)) * esz
            assert off % 4 == 0 and n % 4 == 0
            assert off + n <= reg.shape[1] * 4, (off, n, reg.shape)
            ap = reg[:, off // 4:(off + n) // 4]
            if dt != F32:
                ap = ap.bitcast(dt)
            if len(shape) == 2:
                return ap.rearrange("p (a b) -> p a b", a=shape[0])
            if len(shape) == 3:
                return ap.rearrange("p (a b c) -> p a b c", a=shape[0], b=shape[1])
            return ap

        sem_names = ["pe", "act", "dve", "pool", "ring0", "ring1", "xl0", "xl1", "st0", "st1", "st2", "st3", "st4", "cwb", "wsc0", "wsc1", "wsc2", "wsc3", "wsc4", "wsc5", "wsc6", "wsc7", "wsc8", "wsc9", "wsc10", "misc", "miscp", "wfo", "wfo2", "ctxk", "ctxv0", "ctxv1", "mb", "mn"]
        sems = {n: es.enter_context(nc.semaphore(n)) for n in sem_names}
        block = es.enter_context(nc.Block())
        PE = Stream("pe", sems["pe"], True)
        ACT = Stream("act", sems["act"], False)
        DVE = Stream("dve", sems["dve"], False)
        POOL = Stream("pool", sems["pool"], False)
        SP = Stream("sp", None, False)
        dsem = {n: [sems[n], 0] for n in sem_names[4:]}
        dep = Dep()

        def ev(st, fn, r=(), w=(), extra=()):
            tok = st.op(fn, dep.get(r, w) + list(extra))
            dep.commit(tok, r, w)
            return tok

        def peg(fns, r=(), w=(), extra=()):
            waits = dep.get(r, w) + list(extra)
            tok = None
            n = len(fns)
            for i, fn in enumerate(fns):
                tok = PE.op(fn, waits if i == 0 else (), signal=(i == n - 1))
            dep.commit(tok, r, w)
            return tok

        def dm(st, fns, semc, r=(), w=(), extra=()):
            waits = dep.get(r, w) + list(extra)
            tok = None
            for fn in fns:
                tok = st.dma(fn, semc, waits)
            dep.commit(tok, r, w)
            return tok

        def barrier():
            toks = [PE.last, ACT.last, DVE.last]
            for s in (PE, ACT, DVE, SP):
                s.pending.extend(toks)
            return toks

        o = 0

        def cst(shape, dt):
            nonlocal o
            esz = 2 if dt == BF16 else 4
            n = int(np.prod(shape)) * esz
            n = (n + 3) // 4 * 4
            v = view(CST, o, shape, dt)
            o += n
            return v
        ident = cst([128], BF16)
        zeros = cst([128], F32)
        csil = cst([2, 8, 128], BF16)
        craw = cst([2, 8], F32)
        swb = cst([128], F32)
        lamb = cst([4, 64], F32)
        lamt = cst([8], F32)
        pscale = cst([4], F32)
        cw = cst([3, NFC], F32)
        cb = cst([NFC], F32)
        ect = cst([64], F32)
        wpool = cst([4, 128], BF16)
        ss_t = cst([16], F32)
        rs_t = cst([16], F32)
        att_s = cst([2, 8], F32)
        Pm = cst([128], BF16)
        epsb = cst([1], F32)
        assert o <= 9 * KB, o

        misc = dsem["misc"]
        miscp = dsem["miscp"]
        dm(POOL, [lambda e: e.dma_start(out=ident, in_=c_ident.ap()),
                  lambda e: e.dma_start(out=wpool, in_=w_pool.ap().rearrange("g c e -> c g e")),
                  lambda e: e.dma_start(out=Pm, in_=c_pm.ap())], miscp, w=["ident", "wpool", "Pm"])
        dm(SP, [lambda e: e.dma_start(out=craw, in_=cc.ap().rearrange("a (k p) -> p a k", p=128), allow_slow_non_contiguous=True),
                lambda e: e.dma_start(out=swb, in_=bass.AP(subln_w, 0, [[0, 128], [1, 128]])),
                lambda e: e.dma_start(out=lamb.rearrange("p a b -> p (a b)"), in_=bass.AP(lamv, 0, [[0, 128], [1, 256]])),
                lambda e: e.dma_start(out=ect, in_=c_ect.ap())], misc,
           w=["craw", "swb", "lamb", "ect"])

        def late_param_loads():
            dm(SP, [lambda e: e.dma_start(out=pscale, in_=pool_scale.ap().rearrange("(g p) -> p g", p=128), allow_slow_non_contiguous=True),
                    lambda e: e.dma_start(out=cw, in_=conv_w.ap().rearrange("j (f p) -> p j f", p=128), allow_slow_non_contiguous=True),
                    lambda e: e.dma_start(out=cb, in_=conv_b.ap().rearrange("(f p) -> p f", p=128), allow_slow_non_contiguous=True)], dsem["cwb"],
               w=["pscale", "cw", "cb"])

        ev(DVE, lambda e: e.memset(zeros, 0.0), w=["zeros"])
        ev(DVE, lambda e: e.memset(epsb, EPS), w=["epsb"])
        ev(ACT, lambda e: e.activation(out=craw, in_=craw, func=AF.Silu), r=["craw"], w=["craw"])
        for a in range(2):
            for k in range(8):
                ev(DVE, lambda e, a=a, k=k: e.tensor_scalar(out=csil[:, a, k, :], in0=zeros, scalar1=craw[:, a, k:k + 1], scalar2=None, op0=ALU.add),
                   r=["zeros", "craw"], w=["csil%d_%d" % (a, k)])
        ev(ACT, lambda e: e.activation(out=swb, in_=swb, func=AF.Copy, scale=1.0 - LAM_INIT), r=["swb"], w=["swb"])
        ev(DVE, lambda e: e.tensor_tensor(out=lamb[:, 0, :], in0=lamb[:, 0, :], in1=lamb[:, 1, :], op=ALU.mult), r=["lamb"], w=["lamb"])
        ev(DVE, lambda e: e.tensor_tensor(out=lamb[:, 2, :], in0=lamb[:, 2, :], in1=lamb[:, 3, :], op=ALU.mult), r=["lamb"], w=["lamb"])
        ev(DVE, lambda e: e.tensor_reduce(out=lamt[:, 0:1], in_=lamb[:, 0, :], axis=AX.X, op=ALU.add), r=["lamb"], w=["lamt"])
        ev(DVE, lambda e: e.tensor_reduce(out=lamt[:, 1:2], in_=lamb[:, 2, :], axis=AX.X, op=ALU.add), r=["lamb"], w=["lamt"])
        ev(ACT, lambda e: e.activation(out=lamt[:, 2:4], in_=lamt[:, 0:2], func=AF.Exp), r=["lamt"], w=["lamt"])
        ev(DVE, lambda e: e.tensor_tensor(out=lamt[:, 4:5], in0=lamt[:, 2:3], in1=lamt[:, 3:4], op=ALU.subtract), r=["lamt"], w=["lamt"])
        ev(DVE, lambda e: e.tensor_scalar(out=lamt[:, 5:6], in0=lamt[:, 4:5], scalar1=LAM_INIT, scalar2=-1.0, op0=ALU.add, op1=ALU.mult), r=["lamt"], w=["lamt"])
        neglam = lamt[:, 5:6]
        CSIL = ["csil%d_%d" % (a, k) for a in range(2) for k in range(8)]

        ring_views = [view(RING, 0, [4096], BF16), view(RING, 8 * KB, [4096], BF16)]
        ring_i = [0]

        def ring_load(pieces, extra=(), r=()):
            i = ring_i[0] % 2
            ring_i[0] += 1
            sv = ring_views[i]
            fns = [(lambda e, d=dst_fn(sv), s=src: e.dma_start(out=d, in_=s)) for dst_fn, src in pieces]
            dm(POOL, fns, dsem["ring%d" % i], r=r, w=["ring%d" % i], extra=extra)
            return sv, "ring%d" % i

        def wsrc(wt, c0, ncols, k0=0, nk=8):
            return wt.ap()[k0 * 128:(k0 + nk) * 128, c0:c0 + ncols].rearrange("(k p) c -> p k c", p=128)

        def slab3(sv, off, nk, ncols):
            return sv[:, off:off + nk * ncols].rearrange("p (k c) -> p k c", k=nk)

        modv = [view(MOD, 0, [1024], F32), view(MOD, 4 * KB, [1024], F32)]
        modscr = nc.dram_tensor("modscr", [2, 6 * D], F32, kind="Internal")
        wfi_scr = nc.dram_tensor("wfi_scr", [NFC // 2, 128, 4096], BF16, kind="Internal")

        def prefetch_wfi(fc2):
            dst = wfi_scr.ap()[fc2].rearrange("p (k c) -> p k c", k=8)
            dm(POOL, [lambda e: e.dma_start(out=dst[:, :, 0:256], in_=wsrc(w_ffn_in, fc2 * 256, 256)),
                      lambda e: e.dma_start(out=dst[:, :, 256:512], in_=wsrc(w_ffn_in, DFF + fc2 * 256, 256))], dsem["wsc%d" % fc2], w=["wsc%d" % fc2])
        csil2 = cst([8, 2], BF16)
        ev(DVE, lambda e: e.tensor_copy(out=csil2, in_=craw.rearrange("p a k -> p k a")), r=["craw"], w=["csil2"])
        mr_bb = view(RC, 48 * KB, [1024], F32)
        mr_nw = view(RC, 52 * KB, [1024], F32)
        mr_row = [view(RC, 56 * KB + b * 2 * KB, [512], F32) for b in range(2)]
        NORMW = {1: norm1_w, 4: norm2_w}
        mri = [0]

        mr_bb2 = view(RC, 60 * KB, [1024], F32)

        def start_modrows(j, use_ring=True, extra=(), alt=False):
            bbv, bbn, bbs = (mr_bb2, "mr_bb2", "cwb") if alt else (mr_bb, "mr_bb", "mb")
            dm(SP, [lambda e: e.dma_start(out=bbv[0:2, :], in_=bass.AP(b_ada, j * 1024, [[0, 2], [1, 1024]]))], dsem[bbs], w=[bbn])
            if j in NORMW:
                dm(SP, [lambda e: e.dma_start(out=mr_nw[0:2, :], in_=bass.AP(NORMW[j], 0, [[0, 2], [1, 1024]]))], dsem["mn"], w=["mr_nw"])
            slabs = []
            for hf in range(2):
                if use_ring:
                    slabs.append(ring_load([(lambda s: slab3(s, 0, 8, 512), wsrc(w_ada, j * 1024 + hf * 512, 512))]))
                else:
                    sv = view(RC, 16 * KB + hf * 8 * KB, [4096], BF16)
                    nm = "mslab%d" % hf
                    dm(POOL, [lambda e, sv=sv, hf=hf: e.dma_start(out=slab3(sv, 0, 8, 512), in_=wsrc(w_ada, j * 1024 + hf * 512, 512))], dsem["ctxk" if hf == 0 else "ctxv0"], w=[nm], extra=extra)
                    slabs.append((sv, nm))
            return (j, slabs, bbv, bbn)

        def finish_modrows(st_):
            j, slabs, bbv, bbn = st_
            for hf in range(2):
                sv, rn = slabs[hf]
                wv = slab3(sv, 0, 8, 512)
                bank = hf
                rb = mri[0] % 2
                mri[0] += 1
                row = mr_row[rb]
                peg([(lambda e, k=k, wv=wv, bank=bank: e.matmul(psum[0:2, bank, :], lhsT=csil2[:, k, :], rhs=wv[:, k, :], start=(k == 0), stop=(k == 7))) for k in range(8)],
                    r=[rn, "csil2"], w=["p%d" % bank])
                sl = slice(hf * 512, hf * 512 + 512)
                ev(DVE, lambda e, sl=sl, bank=bank, row=row: e.tensor_tensor(out=row[0:2, :], in0=psum[0:2, bank, :], in1=bbv[0:2, sl], op=ALU.add), r=[bbn], w=["p%d" % bank, "mr_row%d" % rb])
                if j in NORMW:
                    ev(DVE, lambda e, sl=sl, row=row: e.scalar_tensor_tensor(out=row[0:2, :], in0=row[0:2, :], scalar=1.0, in1=mr_nw[0:2, sl], op0=ALU.add, op1=ALU.mult), r=["mr_nw"], w=["mr_row%d" % rb])
                dm(SP, [lambda e, row=row, hf=hf: e.dma_start(out=modscr.ap()[:, j * 1024 + hf * 512:j * 1024 + (hf + 1) * 512], in_=row[0:2, :])], dsem["st3" if rb == 0 else "st4"], r=["mr_row%d" % rb], w=["modscr%d_%d" % (j, hf)])

        def compute_modrows(j):
            finish_modrows(start_modrows(j))

        def load_mod(cond, j, mi, extra=()):
            dm(SP, [lambda e: e.dma_start(out=modv[mi], in_=bass.AP(modscr, cond * 6 * D + j * 1024, [[0, 128], [1, 1024]]))], dsem["mb" if mi == 0 else "mn"],
               r=["modscr%d_0" % j, "modscr%d_1" % j], w=["mod%d" % mi], extra=extra)

        def norm_preload(src_dram, reg, stage_off, n):
            for i in range(n):
                b = i % 2
                xlb = view(reg, stage_off + b * 4 * KB, [1024], F32)
                dm(SP, [lambda e, i=i, xlb=xlb: e.dma_start(out=xlb, in_=src_dram[i * 128:(i + 1) * 128, :])], dsem["xl%d" % b], w=["xl%d" % b])

        def norm_phase(T, src_dram, src_sb, mS, mG, hT, reg, stage_off, hname, stage=None, pre=0):
            nt = T // 128
            xl = [view(reg, stage_off + 0, [1024], F32), view(reg, stage_off + 4 * KB, [1024], F32)]
            tm = [view(reg, stage_off + 8 * KB, [1024], F32), view(reg, stage_off + 12 * KB, [1024], F32)]
            hb = [view(reg, stage_off + 16 * KB, [1024], BF16), view(reg, stage_off + 18 * KB, [1024], BF16)]
            junk = view(reg, stage_off + 20 * KB, [1024], BF16)
            if stage is not None:
                tm, hb, junk = stage
            Sb, Gb = modv[mS], modv[mG]

            def src(i):
                b = i % 2
                if src_dram is not None:
                    return xl[b], ["xl%d" % b]
                return src_sb[:, i, :], ["x1"]

            def stats(i):
                b = i % 2
                if src_dram is not None and i >= pre:
                    dm(SP, [lambda e, i=i, b=b: e.dma_start(out=xl[b], in_=src_dram[i * 128:(i + 1) * 128, :])], dsem["xl%d" % b], w=["xl%d" % b])
                xv, xn = src(i)
                sl_ = slice(i % 16, i % 16 + 1)
                ev(ACT, lambda e, xv=xv, sl_=sl_: e.activation(out=junk, in_=xv, func=AF.Square, accum_out=ss_t[:, sl_]), r=xn, w=["junk", "ss%d" % (i % 16)])
                ev(ACT, lambda e, sl_=sl_: e.activation(out=rs_t[:, sl_], in_=ss_t[:, sl_], func=AF.Sqrt, scale=1.0 / D, bias=EPS), r=["ss%d" % (i % 16)], w=["rs%d" % (i % 16)])

            def mid(i):
                b = i % 2
                xv, xn = src(i)
                sl_ = slice(i % 16, i % 16 + 1)
                ev(DVE, lambda e, sl_=sl_: e.reciprocal(out=rs_t[:, sl_], in_=rs_t[:, sl_]), w=["rs%d" % (i % 16)])
                ev(DVE, lambda e, xv=xv, sl_=sl_, b=b: e.scalar_tensor_tensor(out=tm[b], in0=xv, scalar=rs_t[:, sl_], in1=Gb, op0=ALU.mult, op1=ALU.mult),
                   r=xn + ["rs%d" % (i % 16), "mod%d" % mG], w=["tm%d" % b])
                ev(DVE, lambda e, b=b: e.tensor_tensor(out=hb[b], in0=tm[b], in1=Sb, op=ALU.add), r=["tm%d" % b, "mod%d" % mS], w=["hb%d" % b])

            def fin(i):
                b = i % 2
                bank = 6 + b
                psb = psum[:, bank, :].bitcast(BF16)
                peg([(lambda e, k=k, b=b, psb=psb: e.transpose(out=psb[:, k * 128:(k + 1) * 128], in_=hb[b][:, k * 128:(k + 1) * 128], identity=ident)) for k in range(8)],
                    r=["hb%d" % b, "ident"], w=["p%d" % bank])
                ev(ACT, lambda e, i=i, psb=psb: e.activation(out=hT[:, :, i * 128:(i + 1) * 128], in_=psb.rearrange("p (k t) -> p k t", k=8), func=AF.Copy), w=["p%d" % bank, hname])

            stats(0)
            for i in range(nt):
                if i + 1 < nt:
                    stats(i + 1)
                mid(i)
                fin(i)

        persist = {}

        def run_group(gi, T, nseq, S, ctx, rope, x_dram, y_dram):
            nt = T // 128
            ntb = T // 512
            cond = gi
            hT = view(RB, 0, [8, T], BF16)
            oT = view(RC, 0, [8, T], BF16)
            mergedT = view(RC, 32 * KB, [8, T], BF16)
            x1 = view(RA, 0, [nt, 1024], F32)
            Lk = ctx + S if rope else T
            nkt_all = Lk // 128

            barrier()
            tmpA = view(RA, 0, [1024], F32)
            tmpB = view(RA, 4 * KB, [1024], F32)
            npre = 0
            if gi == 0:
                npre = 2
                norm_preload(x_dram.ap(), RA, 40 * KB, npre)
                st0_ = start_modrows(0, alt=True)
                st1_ = start_modrows(1, use_ring=False)
                finish_modrows(st0_)
                finish_modrows(st1_)
            load_mod(cond, 0, 0)
            load_mod(cond, 1, 1)
            norm_phase(T, x_dram.ap(), None, 0, 1, hT, RA, 40 * KB, "hT", pre=npre)
            barrier()
            _chk('A%d' % gi)
            if gi == 0:
                late_param_loads()

            offB = 0
            qTm = []
            for b in range(2):
                qTm.append([view(RA, offB + (2 * b + m) * T * 2, [T], BF16) for m in range(2)])
            offB += 4 * T * 2
            kT = [view(RA, offB + b * Lk * 2, [Lk], BF16) for b in range(2)]
            offB += 2 * Lk * 2
            vhs = ((nkt_all * 129 * 2 + 3) // 4) * 4
            vh = [view(RA, offB + b * vhs, [nkt_all, 129], BF16) for b in range(2)]
            offB += 2 * vhs
            araws = [view(RA, offB + b * 2064, [2, 258], F32).rearrange("p q (m c) -> p q m c", m=2) for b in range(2)]; offB += 4128
            ckb = view(RA, offB, [2, 128], BF16); offB += 512
            pT = [view(RA, offB + b * 1024, [512], BF16) for b in range(4)]; offB += 4 * KB
            ost = [view(RA, offB + b * 512, [2, 128], BF16) for b in range(2)]; offB += 1 * KB
            att = [[view(RA, offB + (3 * s_ + b) * 1024, [2, 128], F32) for b in range(3)] for s_ in range(2)]; offB += 6 * KB
            rt1 = [view(RA, offB + b * 2 * KB, [512], F32) for b in range(2)]; offB += 4 * KB
            rt2 = [view(RA, offB + b * 2 * KB, [512], F32) for b in range(2)]; offB += 4 * KB
            qraw = [view(RA, offB + b * 1024, [512], BF16) for b in range(2)]; offB += 2 * KB
            kvo = None if rope else [view(RA, offB + b * 4 * KB, [8, 128], F32) for b in range(2)]; offB += (0 if rope else 8 * KB)
            assert offB <= 64 * KB, offB
            cosT = view(RC, 32 * KB, [2048], BF16)
            sinT = view(RC, 36 * KB, [2048], BF16)
            bt = [PE.last, DVE.last, ACT.last]
            if rope:
                dm(POOL, [lambda e: e.dma_start(out=cosT, in_=c_cos.ap()), lambda e: e.dma_start(out=sinT, in_=c_sin.ap())], miscp, w=["rope"], extra=bt)
            for b in range(2):
                for m in range(2):
                    ev(DVE, lambda e, b=b, m=m: e.memset(qTm[b][m], 0.0), w=["qT%d" % b])
                ev(DVE, lambda e, b=b: e.memset(vh[b][:, :, 128:129], 1.0), w=["vh%d" % b])
            SCALE = 64 ** -0.5
            pj = [0]
            qbi = [0]
            gstep = [0]
            dseq = [0]
            dq = []

            def run_due(force=False):
                while dq and (force or dq[0][0] <= gstep[0]):
                    dq.pop(0)[2]()

            def peg_split(fns, r, w, nparts):
                n = len(fns)
                per = n // nparts
                parts = []
                for p in range(nparts):
                    def part(p=p):
                        lo = p * per
                        hi = n if p == nparts - 1 else (p + 1) * per
                        tok = None
                        for i in range(lo, hi):
                            tok = PE.op(fns[i], dep.get(r, w) if i == 0 else (), signal=(i == n - 1))
                        if hi == n:
                            dep.commit(tok, r, w)
                    parts.append(part)
                return parts

            def head_parts(h):
                hb_ = h % 2
                qn, kn, vn = "qT%d" % hb_, "kT%d" % hb_, "vh%d" % hb_
                parts = []
                st = {}

                def u0():
                    sv, rn = ring_load([
                        (lambda s: slab3(s, 0, 8, 384)[:, :, 0:128], wsrc(w_in, 512 + h * 128, 128)),
                        (lambda s: slab3(s, 0, 8, 384)[:, :, 128:256], wsrc(w_in, 1536 + h * 128, 128)),
                        (lambda s: slab3(s, 0, 8, 384)[:, :, 256:384], wsrc(w_in, 2560 + h * 128, 128)),
                    ])
                    st["rn"] = rn
                    st["wq"] = slab3(sv, 0, 8, 384)
                    wq = st["wq"]
                    if ctx:
                        dm(POOL, [lambda e: e.dma_start(out=vh[hb_][:, 0:2, 0:128], in_=cv.ap()[:, h * 128:(h + 1) * 128].rearrange("(t p) c -> p t c", p=128))],
                           dsem["ctxv%d" % hb_], w=[vn], extra=bt)
                        dm(POOL, [lambda e: e.dma_start(out=ckb, in_=ck.ap()[:, h * 128:(h + 1) * 128].rearrange("(t p) c -> p t c", p=128))], dsem["ctxk"], w=["ckb"], extra=bt)
                parts.append(u0)
                if ctx:
                    def u1():
                        psb = psum[:, 7, :].bitcast(BF16)
                        peg([(lambda e, t2=t2, psb=psb: e.transpose(out=psb[:, t2 * 128:(t2 + 1) * 128], in_=ckb[:, t2, :], identity=ident)) for t2 in range(2)], r=["ckb", "ident"], w=["p7"])
                        ev(DVE, lambda e, psb=psb: e.tensor_copy(out=kT[hb_][:, 0:256], in_=psb[:, 0:256]), w=["p7", kn])
                    parts.append(u1)
                for which in range(2):
                    for tb in range(ntb):
                        b0 = 5 if rope else 5 + (pj[0] % 2)
                        rb = pj[0] % 2
                        pj[0] += 1
                        cs = slice(tb * 512, (tb + 1) * 512)

                        def mk_main(which=which, cs=cs, b0=b0):
                            wq = st["wq"]
                            return peg_split([(lambda e, k=k: e.matmul(psum[:, b0, :], lhsT=wq[:, k, which * 128:(which + 1) * 128], rhs=hT[:, k, cs], start=(k == 0), stop=(k == 7))) for k in range(8)],
                                             [st["rn"], "hT"], ["p%d" % b0], 2)

                        def mk_perm(which=which, cs=cs, b0=b0):
                            return peg_split([(lambda e, k=k: e.matmul(psum[:, b0 + 1, :], lhsT=wP[:, k, which, :], rhs=hT[:, k, cs], start=(k == 0), stop=(k == 7))) for k in range(8)],
                                             ["wP", "hT"], ["p%d" % (b0 + 1)], 2)
                        holder = {}
                        for pi in range(2):
                            def part(pi=pi, maker=mk_main, holder=holder):
                                if "p" not in holder:
                                    holder["p"] = maker()
                                holder["p"][pi]()
                            parts.append(part)

                        if rope:
                            def evacA(cs=cs, b0=b0, rb=rb):
                                ev(DVE, lambda e: e.tensor_copy(out=qraw[rb], in_=psum[:, b0, :]), w=["p%d" % b0, "qraw%d" % rb])
                                ev(DVE, lambda e: e.tensor_tensor(out=rt1[rb], in0=psum[:, b0, :], in1=cosT[:, cs], op=ALU.mult), r=["rope"], w=["p%d" % b0, "rt1_%d" % rb])
                            parts.append(evacA)

                            def perm(b0=b0, rb=rb):
                                peg([lambda e: e.matmul(psum[:, b0 + 1, :], lhsT=Pm, rhs=qraw[rb], start=True, stop=True)], r=["qraw%d" % rb, "Pm"], w=["p%d" % (b0 + 1)])
                            parts.append(perm)

                            def evacB(which=which, tb=tb, cs=cs, b0=b0, rb=rb):
                                ev(DVE, lambda e: e.tensor_tensor(out=rt2[rb], in0=psum[:, b0 + 1, :], in1=sinT[:, cs], op=ALU.mult), r=["rope"], w=["p%d" % (b0 + 1), "rt2_%d" % rb])
                                if which == 0:
                                    ev(DVE, lambda e: e.tensor_tensor(out=qTm[hb_][0][0:64, cs], in0=rt1[rb][0:64, :], in1=rt2[rb][0:64, :], op=ALU.add), r=["rt1_%d" % rb, "rt2_%d" % rb], w=[qn])
                                    ev(DVE, lambda e: e.tensor_tensor(out=qTm[hb_][1][64:128, cs], in0=rt1[rb][64:128, :], in1=rt2[rb][64:128, :], op=ALU.add), r=["rt1_%d" % rb, "rt2_%d" % rb], w=[qn])
                                else:
                                    kc_ = slice(ctx + tb * 512, ctx + (tb + 1) * 512)
                                    ev(DVE, lambda e: e.tensor_tensor(out=kT[hb_][:, kc_], in0=rt1[rb], in1=rt2[rb], op=ALU.add), r=["rt1_%d" % rb, "rt2_%d" % rb], w=[kn])
                            parts.append(evacB)
                        else:
                            def evac(which=which, cs=cs, b0=b0):
                                if which == 0:
                                    ev(ACT, lambda e: e.activation(out=qTm[hb_][0][0:64, cs], in_=psum[0:64, b0, :], func=AF.Copy), w=["p%d" % b0, qn])
                                    ev(ACT, lambda e: e.activation(out=qTm[hb_][1][64:128, cs], in_=psum[64:128, b0, :], func=AF.Copy), w=["p%d" % b0, qn])
                                else:
                                    ev(ACT, lambda e: e.activation(out=kT[hb_][:, cs], in_=psum[:, b0, :], func=AF.Copy), w=["p%d" % b0, kn])
                            parts.append(evac)
                outs = [(2, vh[hb_], nv)] + ([] if rope else [(1, None, nk)])
                for (wsel, dst_bf, dst_dram) in outs:
                    for t4 in range(nt // 4):
                        b0 = 5 + (pj[0] % 2)
                        pj[0] += 1

                        def mk_v(wsel=wsel, t4=t4, b0=b0):
                            wq = st["wq"]
                            fns = []
                            for tt in range(4):
                                ti = t4 * 4 + tt
                                for k in range(8):
                                    fns.append(lambda e, k=k, ti=ti, tt=tt: e.matmul(psum[:, b0, tt * 128:(tt + 1) * 128], lhsT=hT[:, k, ti * 128:(ti + 1) * 128], rhs=wq[:, k, wsel * 128:(wsel + 1) * 128], start=(k == 0), stop=(k == 7)))
                            return peg_split(fns, [st["rn"], "hT"], ["p%d" % b0], 4)
                        holder = {}
                        for pi in range(4):
                            def part(pi=pi, maker=mk_v, holder=holder):
                                if "p" not in holder:
                                    holder["p"] = maker()
                                holder["p"][pi]()
                            parts.append(part)

                        def evac_v(wsel=wsel, dst_bf=dst_bf, dst_dram=dst_dram, t4=t4, b0=b0):
                            if dst_bf is not None:
                                kt0 = (ctx // 128) + t4 * 4
                                if rope:
                                    ev(DVE, lambda e: e.tensor_copy(out=dst_bf[:, kt0:kt0 + 4, 0:128], in_=psum[:, b0, :].rearrange("p (t c) -> p t c", t=4)), w=["p%d" % b0, vn])
                                else:
                                    ev(ACT, lambda e: e.activation(out=dst_bf[:, kt0:kt0 + 4, 0:128], in_=psum[:, b0, :].rearrange("p (t c) -> p t c", t=4), func=AF.Copy), w=["p%d" % b0, vn])
                            if not rope:
                                ob = kvo[wsel - 1]
                                on = "kvo%d" % (wsel - 1)
                                ev(ACT, lambda e: e.activation(out=ob[:, t4 * 4:(t4 + 1) * 4, :], in_=psum[:, b0, :].rearrange("p (t c) -> p t c", t=4), func=AF.Copy), w=["p%d" % b0, on])
                                if t4 == nt // 4 - 1:
                                    dm(SP, [lambda e: e.dma_start(out=dst_dram.ap()[:, h * 128:(h + 1) * 128].rearrange("(t p) c -> p t c", p=128), in_=ob)],
                                       dsem["st%d" % (wsel - 1)], r=[on])
                        parts.append(evac_v)
                return parts

            pending_parts = []
            for h in range(NH):
                hb_ = h % 2
                qn, kn, vn = "qT%d" % hb_, "kT%d" % hb_, "vh%d" % hb_
                if h == 0:
                    for part in head_parts(h):
                        part()
                else:
                    for part in pending_parts:
                        part()
                pending_parts = head_parts(h + 1) if (h + 1 < NH) else []
                astep = [0]

                mr_state = None
                mr_j = [(h + 1) if (gi == 0 and 1 <= h <= 4) else None]
                if gi == 0:
                    for fc2_ in (h, h + 8):
                        if fc2_ < NFC // 2:
                            prefetch_wfi(fc2_)
                nqb = T // 256
                for qb in range(nqb):
                    ktiles = list(range(nkt_all)) if rope else [2 * qb, 2 * qb + 1]
                    nk_ = len(ktiles)
                    qs = slice(qb * 256, (qb + 1) * 256)
                    es_ = qbi[0] % 2
                    qbi[0] += 1
                    abase = 3
                    ACC = ["p%d" % abase, "p%d" % (abase + 1)]

                    def emit_av(ki, kt, hb_=hb_, vn=vn, nk_=nk_, abase=abase, ACC=ACC):
                        pb = ki % 4
                        first, last = (ki == 0), (ki == nk_ - 1)
                        fns = []
                        for qt in range(2):
                            for m in range(2):
                                bank = abase + qt
                                st_ = first and (m == 0)
                                fns.append(lambda e, pb=pb, kt=kt, qt=qt, m=m, bank=bank, st_=st_, last=last, hb_=hb_: e.matmul(
                                    psum[:, bank, m * 256:m * 256 + 129], lhsT=pT[pb][:, m * 256 + qt * 128:m * 256 + (qt + 1) * 128], rhs=vh[hb_][:, kt, :], start=st_, stop=last, skip_group_check=True))
                        peg(fns, r=["pT%d" % pb, vn], w=ACC)

                    for ki, kt in enumerate(ktiles):
                        sb_ = ki % 3
                        pb = ki % 4
                        peg([(lambda e, m=m, kt=kt, sb_=sb_, qs=qs, hb_=hb_: e.matmul(psum[:, sb_, m * 256:(m + 1) * 256], lhsT=kT[hb_][:, kt * 128:(kt + 1) * 128], rhs=qTm[hb_][m][:, qs], start=True, stop=True)) for m in range(2)],
                            r=[qn, kn], w=["p%d" % sb_])
                        ev(ACT, lambda e, sb_=sb_, pb=pb: e.activation(out=pT[pb], in_=psum[:, sb_, :], func=AF.Exp, scale=SCALE), w=["p%d" % sb_, "pT%d" % pb])
                        if ki >= 2:
                            emit_av(ki - 2, ktiles[ki - 2])
                        gstep[0] += 1
                        run_due()
                        astep[0] += 1
                        if rope:
                            if pending_parts and astep[0] >= 3 and astep[0] % 2 == 0:
                                pending_parts.pop(0)()
                        else:
                            for _ in range(5):
                                if pending_parts:
                                    pending_parts.pop(0)()
                            if mr_j[0] is not None:
                                mr_state = start_modrows(mr_j[0], use_ring=False, extra=bt)
                                mr_j[0] = None
                    for ki in range(max(0, nk_ - 2), nk_):
                        emit_av(ki, ktiles[ki])
                    a0, a1, a2 = att[es_]
                    an = "att%d" % es_
                    sc = att_s[:, es_, :]
                    scn = "atts%d" % es_
                    accv = psum[:, abase:abase + 2, :].rearrange("p q (m c) -> p q m c", m=2)
                    araw = araws[es_]
                    arn = "araw%d" % es_
                    ev(DVE, lambda e, accv=accv, araw=araw: e.tensor_copy(out=araw, in_=accv[:, :, :, 0:129]), w=ACC + [arn])
                    ev(DVE, lambda e, araw=araw, sc=sc: e.reciprocal(out=sc[:, 0:4].rearrange("p (q m) -> p q m", m=2), in_=araw[:, :, :, 128]), r=[arn], w=[scn])
                    rzv = sc[:, 0:4].rearrange("p (q m) -> p q m", m=2)
                    ev(DVE, lambda e, rzv=rzv: e.tensor_scalar(out=rzv[:, :, 1], in0=rzv[:, :, 1], scalar1=neglam, scalar2=None, op0=ALU.mult), r=["lamt"], w=[scn])
                    O = araw[:, :, :, 0:128]
                    ev(DVE, lambda e, O=O, rzv=rzv, a0=a0: e.tensor_tensor(out=a0, in0=O[:, :, 1, :], in1=bass.AP(rzv.tensor, rzv.offset + 1, [list(rzv.ap[0]), [2, 2], [0, 128]]), op=ALU.mult), r=[scn, arn], w=[an + "a"])
                    ev(DVE, lambda e, O=O, rzv=rzv, a1=a1: e.tensor_tensor(out=a1, in0=O[:, :, 0, :], in1=bass.AP(rzv.tensor, rzv.offset, [list(rzv.ap[0]), [2, 2], [0, 128]]), op=ALU.mult), r=[scn, arn], w=[an + "b"])
                    ev(DVE, lambda e, a0=a0, a1=a1: e.tensor_tensor(out=a1, in0=a1, in1=a0, op=ALU.add), r=[an + "a"], w=[an + "b"])
                    ev(DVE, lambda e, a1=a1, a2=a2: e.tensor_tensor(out=a2, in0=a1, in1=a1, op=ALU.mult), r=[an + "b"], w=[an + "c"])
                    ev(DVE, lambda e, a2=a2, sc=sc: e.tensor_reduce(out=sc[:, 4:6], in_=a2, axis=AX.X, op=ALU.add), r=[an + "c"], w=[scn])

                    def stage1(sc=sc, scn=scn, a1=a1, an=an, es_=es_):
                        ev(ACT, lambda e: e.activation(out=sc[:, 6:8], in_=sc[:, 4:6], func=AF.Ln, scale=1.0 / 128, bias=epsb), r=["epsb"], w=[scn])
                        ev(ACT, lambda e: e.activation(out=sc[:, 6:8], in_=sc[:, 6:8], func=AF.Exp, scale=-0.5), w=[scn])
                        rsv = sc[:, 6:8]
                        ev(DVE, lambda e: e.tensor_tensor(out=a1, in0=a1, in1=bass.AP(rsv.tensor, rsv.offset, [list(rsv.ap[0]), [1, 2], [0, 128]]), op=ALU.mult), r=[scn], w=[an + "b"])
                        ev(DVE, lambda e: e.tensor_tensor(out=ost[es_], in0=a1, in1=bass.AP(swb.tensor, swb.offset, [list(swb.ap[0]), [0, 2], [1, 128]]), op=ALU.mult), r=[an + "b", "swb"], w=["ost%d" % es_])

                    def stage2(es_=es_):
                        psb = psum[:, 7, :].bitcast(BF16)
                        peg([(lambda e, qt=qt: e.transpose(out=psb[:, qt * 128:(qt + 1) * 128], in_=ost[es_][:, qt, :], identity=ident)) for qt in range(2)], r=["ost%d" % es_, "ident"], w=["p7"])

                    def stage3(h=h, qs=qs):
                        psb = psum[:, 7, :].bitcast(BF16)
                        ev(ACT, lambda e: e.activation(out=oT[:, h, qs], in_=psb[:, 0:256], func=AF.Copy), w=["p7", "oT"])

                    for dly, fn_ in (((8, stage1), (11, stage2), (12, stage3)) if rope else ((4, stage1), (7, stage2), (8, stage3))):
                        dq.append((gstep[0] + dly, dseq[0], fn_))
                        dseq[0] += 1
                    dq.sort(key=lambda t: (t[0], t[1]))
                if mr_state is not None:
                    finish_modrows(mr_state)
            run_due(force=True)
            if not rope:
                for s_ in (PE, ACT, DVE):
                    s_.pending.extend(dep.get([], ["kvo0", "kvo1", "mr_row0", "mr_row1", "mr_bb", "mr_bb2", "mr_nw", "mslab0", "mslab1"]))
            barrier()
            _chk('B%d' % gi)

            W = S + 16
            upads = [view(RA, 0, [nseq, W], F32), view(RA, 48 * KB, [nseq, W], F32)]
            sA = view(RA, 9 * KB, [nseq, W], F32)
            sB = view(RA, 18 * KB, [nseq, W], F32)
            pooled = [view(RA, 27 * KB, [T], BF16), view(RA, 60 * KB, [T], BF16)]
            mixedT = view(RA, 32 * KB, [4, T], BF16)
            sg = [[view(RA, 48 * KB + (2 * s_ + j) * 2 * KB, [512], F32) for j in range(2)] for s_ in range(2)]
            ev(DVE, lambda e: e.memset(upads[0], 0.0), w=["upad0"])
            ev(DVE, lambda e: e.memset(upads[1], 0.0), w=["upad1"])
            ev(DVE, lambda e: e.memset(sA, 0.0), w=["sA"])
            ev(DVE, lambda e: e.memset(sB, 0.0), w=["sB"])
            etmp = view(FS, 0, [64], F32)
            sv, rn = ring_load([(lambda s: slab3(s, 0, 8, 512), wsrc(w_in, 0, 512))])
            wu = slab3(sv, 0, 8, 512)
            spb = 512 // S if S < 512 else 1
            def c1_proj(g):
                up = upads[g % 2]
                un = "upad%d" % (g % 2)
                for tb in range(ntb):
                    bank = tb % 2
                    peg([(lambda e, k=k, g=g, tb=tb, bank=bank: e.matmul(psum[:, bank, :], lhsT=wu[:, k, g * 128:(g + 1) * 128], rhs=hT[:, k, tb * 512:(tb + 1) * 512], start=(k == 0), stop=(k == 7))) for k in range(8)],
                        r=[rn, "hT"], w=["p%d" % bank])
                    if S >= 512:
                        dstu = up[:, 0, 8 + tb * 512:8 + (tb + 1) * 512]
                        srcu = psum[:, bank, :]
                    else:
                        dstu = up[:, tb * spb:(tb + 1) * spb, 8:8 + S]
                        srcu = psum[:, bank, :].rearrange("p (s t) -> p s t", s=spb)
                    ev(ACT, lambda e, dstu=dstu, srcu=srcu: e.activation(out=dstu, in_=srcu, func=AF.Copy), w=["p%d" % bank, un])

            def c1_pool(g):
                up = upads[g % 2]
                un = "upad%d" % (g % 2)
                pl = pooled[g % 2]
                pn = "pooled%d" % (g % 2)
                w = 2 << g
                ev(DVE, lambda e: e.tensor_tensor(out=sA[:, :, 1:W], in0=up[:, :, 0:W - 1], in1=up[:, :, 1:W], op=ALU.add), r=[un], w=["sA"])
                cur = sA; cn = "sA"
                if w >= 4:
                    ev(DVE, lambda e: e.tensor_tensor(out=sB[:, :, 2:W - 1], in0=sA[:, :, 1:W - 2], in1=sA[:, :, 3:W], op=ALU.add), r=["sA"], w=["sB"])
                    cur = sB; cn = "sB"
                if w >= 8:
                    ev(DVE, lambda e: e.tensor_tensor(out=sA[:, :, 4:W - 3], in0=sB[:, :, 2:W - 5], in1=sB[:, :, 6:W - 1], op=ALU.add), r=["sB"], w=["sA"])
                    cur = sA; cn = "sA"
                if w >= 16:
                    ev(DVE, lambda e: e.tensor_tensor(out=sB[:, :, 8:W - 7], in0=sA[:, :, 4:W - 11], in1=sA[:, :, 12:W - 3], op=ALU.add), r=["sA"], w=["sB"])
                    cur = sB; cn = "sB"
                pv = pl.rearrange("p (s t) -> p s t", s=nseq)
                ev(DVE, lambda e, cur=cur, w=w, pv=pv: e.scalar_tensor_tensor(out=pv, in0=cur[:, :, 8:8 + S], scalar=1.0 / w, in1=up[:, :, 8:8 + S], op0=ALU.mult, op1=ALU.subtract), r=[cn, un], w=[pn])
                hw_ = w // 2
                eL = bass.AP(ect.tensor, ect.offset + g * 16, [list(ect.ap[0]), [0, nseq], [1, hw_]])
                ev(DVE, lambda e, cur=cur, eL=eL, hw_=hw_: e.tensor_tensor(out=etmp[:, 0:nseq * hw_].rearrange("p (s t) -> p s t", s=nseq), in0=cur[:, :, 8:8 + hw_], in1=eL, op=ALU.mult), r=[cn, "ect"], w=["etmp"])
                ev(DVE, lambda e, pv=pv, hw_=hw_: e.tensor_tensor(out=pv[:, :, 0:hw_], in0=etmp[:, 0:nseq * hw_].rearrange("p (s t) -> p s t", s=nseq), in1=up[:, :, 8:8 + hw_], op=ALU.subtract), r=["etmp", un], w=[pn])
                if hw_ > 1:
                    nr = hw_ - 1
                    eR = bass.AP(ect.tensor, ect.offset + g * 16 + 8, [list(ect.ap[0]), [0, nseq], [1, nr]])
                    r0 = S - hw_ + 1
                    ev(DVE, lambda e, cur=cur, eR=eR, nr=nr, r0=r0: e.tensor_tensor(out=etmp[:, 0:nseq * nr].rearrange("p (s t) -> p s t", s=nseq), in0=cur[:, :, 8 + r0:8 + r0 + nr], in1=eR, op=ALU.mult), r=[cn, "ect"], w=["etmp"])
                    ev(DVE, lambda e, pv=pv, nr=nr, r0=r0: e.tensor_tensor(out=pv[:, :, r0:r0 + nr], in0=etmp[:, 0:nseq * nr].rearrange("p (s t) -> p s t", s=nseq), in1=up[:, :, 8 + r0:8 + r0 + nr], op=ALU.subtract), r=["etmp", un], w=[pn])
                for tb in range(ntb):
                    bank = 2 + tb % 2
                    peg([lambda e, g=g, tb=tb, bank=bank, pl=pl: e.matmul(psum[:, bank, :], lhsT=wpool[:, g, :], rhs=pl[:, tb * 512:(tb + 1) * 512], start=True, stop=True)],
                        r=[pn, "wpool"], w=["p%d" % bank])
                    ev(ACT, lambda e, g=g, tb=tb, bank=bank: e.activation(out=mixedT[:, g, tb * 512:(tb + 1) * 512], in_=psum[:, bank, :], func=AF.Copy, scale=pscale[:, g:g + 1]), r=["pscale"], w=["p%d" % bank, "mixedT"])

            c1_proj(0)
            for g in range(4):
                if g + 1 < 4:
                    c1_proj(g + 1)
                c1_pool(g)
            barrier()

            it = 0
            for mo in range(8):
                sv, rn = ring_load([
                    (lambda s: slab3(s, 0, 8, 256)[:, :, 0:128], wsrc(w_in, 3584 + mo * 128, 128)),
                    (lambda s: slab3(s, 0, 8, 256)[:, :, 128:256], wsrc(w_in, 3584 + 1024 + mo * 128, 128)),
                    (lambda s: slab3(s, 2048, 4, 128), wsrc(w_proj_a, mo * 128, 128, 0, 4)),
                    (lambda s: slab3(s, 2560, 8, 128), wsrc(w_proj_b, mo * 128, 128)),
                ])
                wg = slab3(sv, 0, 8, 256)
                wa = slab3(sv, 2048, 4, 128)
                wb = slab3(sv, 2560, 8, 128)
                for tb in range(ntb):
                    s_ = it % 2
                    b0 = 4 * s_
                    it += 1
                    cs = slice(tb * 512, (tb + 1) * 512)
                    for j in range(2):
                        peg([(lambda e, k=k, j=j, b0=b0, cs=cs, wg=wg: e.matmul(psum[:, b0 + j, :], lhsT=wg[:, k, j * 128:(j + 1) * 128], rhs=hT[:, k, cs], start=(k == 0), stop=(k == 7))) for k in range(8)],
                            r=[rn, "hT"], w=["p%d" % (b0 + j)])
                    peg([(lambda e, k=k, b0=b0, cs=cs, wa=wa: e.matmul(psum[:, b0 + 2, :], lhsT=wa[:, k, :], rhs=mixedT[:, k, cs], start=(k == 0), stop=(k == 3))) for k in range(4)],
                        r=[rn, "mixedT"], w=["p%d" % (b0 + 2)])
                    peg([(lambda e, k=k, b0=b0, cs=cs, wb=wb: e.matmul(psum[:, b0 + 3, :], lhsT=wb[:, k, :], rhs=oT[:, k, cs], start=(k == 0), stop=(k == 7))) for k in range(8)],
                        r=[rn, "oT"], w=["p%d" % (b0 + 3)])
                    g0, g1 = sg[s_]
                    ev(ACT, lambda e, b0=b0, g0=g0: e.activation(out=g0, in_=psum[:, b0, :], func=AF.Sigmoid), w=["p%d" % b0, "sg%d_0" % s_])
                    ev(ACT, lambda e, b0=b0, g1=g1: e.activation(out=g1, in_=psum[:, b0 + 1, :], func=AF.Sigmoid), w=["p%d" % (b0 + 1), "sg%d_1" % s_])
                    ev(DVE, lambda e, b0=b0, g0=g0: e.tensor_tensor(out=g0, in0=psum[:, b0 + 2, :], in1=g0, op=ALU.mult), w=["p%d" % (b0 + 2), "sg%d_0" % s_])
                    ev(DVE, lambda e, b0=b0, g1=g1: e.tensor_tensor(out=g1, in0=psum[:, b0 + 3, :], in1=g1, op=ALU.mult), w=["p%d" % (b0 + 3), "sg%d_1" % s_])
                    ev(DVE, lambda e, mo=mo, cs=cs, g0=g0, g1=g1: e.tensor_tensor(out=mergedT[:, mo, cs], in0=g0, in1=g1, op=ALU.add), r=["sg%d_0" % s_, "sg%d_1" % s_], w=["mergedT"])
            barrier()
            _chk('C%d' % gi)

            load_mod(cond, 2, 0)
            for s_ in (PE, ACT, DVE, SP):
                s_.pending.extend(dep.get([], ["xo"]))
            xh = [view(FS, b * 2 * KB, [512], F32) for b in range(2)]
            dt_ = [view(FS, 4 * KB + b * 2 * KB, [512], F32) for b in range(4)]
            wfo = view(RC, 0, [NFC, 1024], BF16)
            btD = [PE.last, DVE.last, ACT.last]
            slabD = [ring_load([(lambda s, hf=hf: slab3(s, 0, 8, 512), wsrc(w_out, hf * 512, 512))]) for hf in range(2)]
            dm(POOL, [(lambda e, k=k: e.dma_start(out=wfo[:, k, :], in_=w_ffn_out.ap()[k * 128:(k + 1) * 128, :])) for k in range(16)], dsem["wfo"], w=["wfoA"], extra=btD)
            it = 0
            for hf in range(2):
                sv, rn = slabD[hf]
                wo = slab3(sv, 0, 8, 512)
                hs = slice(hf * 512, (hf + 1) * 512)
                for i in range(nt):
                    b = it % 2
                    b4 = it % 4
                    it += 1
                    dm(SP, [lambda e, i=i, b=b, hs=hs: e.dma_start(out=xh[b], in_=x_dram.ap()[i * 128:(i + 1) * 128, hs])], dsem["xl%d" % b], w=["xh%d" % b])
                    peg([(lambda e, k=k, i=i, b4=b4, wo=wo: e.matmul(psum[:, 2 + b4, :], lhsT=mergedT[:, k, i * 128:(i + 1) * 128], rhs=wo[:, k, :], start=(k == 0), stop=(k == 7))) for k in range(8)],
                        r=[rn, "mergedT"], w=["p%d" % (2 + b4)])
                    ev(DVE, lambda e, b4=b4, hs=hs: e.tensor_tensor(out=dt_[b4], in0=psum[:, 2 + b4, :], in1=modv[0][:, hs], op=ALU.mult), r=["mod0"], w=["p%d" % (2 + b4), "dt%d" % b4])
                    ev(DVE, lambda e, b=b, b4=b4, hs=hs, i=i: e.tensor_tensor(out=x1[:, i, hs], in0=dt_[b4], in1=xh[b], op=ALU.add), r=["dt%d" % b4, "xh%d" % b], w=["x1"])
            barrier()
            _chk('D%d' % gi)

            load_mod(cond, 3, 0)
            load_mod(cond, 4, 1)
            h2T = hT
            btE = [PE.last, DVE.last, ACT.last]
            dm(POOL, [(lambda e, k=k: e.dma_start(out=wfo[:, k, :], in_=w_ffn_out.ap()[k * 128:(k + 1) * 128, :])) for k in range(16, NFC)], dsem["wfo2"], w=["wfoB"], extra=btE)
            stageE = ([view(FS, 0, [1024], F32), view(FS, 4 * KB, [1024], F32)],
                      [view(FS, 8 * KB, [1024], BF16), view(FS, 10 * KB, [1024], BF16)],
                      view(RC, 64 * KB, [1024], BF16))
            norm_phase(T, None, x1, 0, 1, h2T, RC, 0, "hT", stage=stageE)
            barrier()
            _chk('E%d' % gi)

            load_mod(cond, 5, 0)
            bt = [PE.last, DVE.last, ACT.last]
            dm(SP, [lambda e: e.dma_start(out=modv[1], in_=bass.AP(fnw, 0, [[0, 128], [1, 1024]]))], misc, w=["mod1"], extra=bt)
            gT = view(RC, 44 * KB, [NFC, 512], BF16)
            c0s = [view(FS, s_ * 4 * KB, [512], F32) for s_ in range(2)]
            c1s = [view(FS, s_ * 4 * KB + 2 * KB, [512], F32) for s_ in range(2)]
            xo = view(FS, 8 * KB, [1024], F32)
            itf = 0
            f1_pend = [None]
            for tb in range(ntb):
                t0 = tb * 512
                for fc2 in range(NFC // 2):
                    sv, rn = ring_load([(lambda s: s, wfi_scr.ap()[fc2])], r=["wsc%d" % fc2])
                    wf = slab3(sv, 0, 8, 512)
                    for f2 in range(2):
                        fc = fc2 * 2 + f2
                        s_ = itf % 2
                        itf += 1
                        ba, bu = 2 * s_, 2 * s_ + 1
                        c0, c1 = c0s[s_], c1s[s_]
                        c0n, c1n = "c0_%d" % s_, "c1_%d" % s_
                        peg([(lambda e, k=k, f2=f2, ba=ba, t0=t0, wf=wf: e.matmul(psum[:, ba, :], lhsT=wf[:, k, f2 * 128:(f2 + 1) * 128], rhs=h2T[:, k, t0:t0 + 512], start=(k == 0), stop=(k == 7))) for k in range(8)],
                            r=[rn, "hT"], w=["p%d" % ba])
                        peg([(lambda e, k=k, f2=f2, bu=bu, t0=t0, wf=wf: e.matmul(psum[:, bu, :], lhsT=wf[:, k, 256 + f2 * 128:256 + (f2 + 1) * 128], rhs=h2T[:, k, t0:t0 + 512], start=(k == 0), stop=(k == 7))) for k in range(8)],
                            r=[rn, "hT"], w=["p%d" % bu])
                        need_l = (t0 % S) != 0
                        need_r = ((t0 + 512) % S) != 0
                        cols = ([t0 - 1] if need_l else []) + ([t0 + 512] if need_r else [])
                        if cols:
                            fns = []
                            for ci, col in enumerate(cols):
                                for k in range(8):
                                    fns.append(lambda e, k=k, f2=f2, col=col, ci=ci, s_=s_, wf=wf: e.matmul(psum[:, 4, 2 * s_ + ci:2 * s_ + ci + 1], lhsT=wf[:, k, f2 * 128:(f2 + 1) * 128], rhs=h2T[:, k, col:col + 1], start=(k == 0), stop=(k == 7)))
                            peg(fns, r=[rn, "hT"], w=["p4"])
                        ev(ACT, lambda e, fc=fc, ba=ba, c0=c0: e.activation(out=c0, in_=psum[:, ba, :], func=AF.Identity, scale=cw[:, 1, fc:fc + 1], bias=cb[:, fc:fc + 1]), r=["cw", "cb"], w=["p%d" % ba, c0n])
                        nsq = 512 // S if S < 512 else 1
                        Sq = 512 // nsq
                        Av = psum[:, ba, :].rearrange("p (s t) -> p s t", s=nsq)
                        c0v = c0.rearrange("p (s t) -> p s t", s=nsq)
                        ev(DVE, lambda e, fc=fc, Av=Av, c0v=c0v, Sq=Sq: e.scalar_tensor_tensor(out=c0v[:, :, 1:Sq], in0=Av[:, :, 0:Sq - 1], scalar=cw[:, 0, fc:fc + 1], in1=c0v[:, :, 1:Sq], op0=ALU.mult, op1=ALU.add), r=["cw"], w=["p%d" % ba, c0n])
                        ev(DVE, lambda e, fc=fc, Av=Av, c0v=c0v, Sq=Sq: e.scalar_tensor_tensor(out=c0v[:, :, 0:Sq - 1], in0=Av[:, :, 1:Sq], scalar=cw[:, 2, fc:fc + 1], in1=c0v[:, :, 0:Sq - 1], op0=ALU.mult, op1=ALU.add), r=["cw"], w=["p%d" % ba, c0n])
                        if need_l:
                            ev(DVE, lambda e, fc=fc, c0=c0, s_=s_: e.scalar_tensor_tensor(out=c0[:, 0:1], in0=psum[:, 4, 2 * s_:2 * s_ + 1], scalar=cw[:, 0, fc:fc + 1], in1=c0[:, 0:1], op0=ALU.mult, op1=ALU.add), r=["cw"], w=["p4", c0n])
                        if need_r:
                            ci = 1 if need_l else 0
                            ev(DVE, lambda e, fc=fc, c0=c0, s_=s_, ci=ci: e.scalar_tensor_tensor(out=c0[:, 511:512], in0=psum[:, 4, 2 * s_ + ci:2 * s_ + ci + 1], scalar=cw[:, 2, fc:fc + 1], in1=c0[:, 511:512], op0=ALU.mult, op1=ALU.add), r=["cw"], w=["p4", c0n])
                        def f1_tail(fc=fc, bu=bu, c0=c0, c1=c1, c0n=c0n, c1n=c1n):
                            ev(ACT, lambda e: e.activation(out=c1, in_=c0, func=AF.Silu), r=[c0n], w=[c1n])
                            ev(DVE, lambda e: e.tensor_tensor(out=gT[:, fc, :], in0=psum[:, bu, :], in1=c1, op=ALU.mult), r=[c1n], w=["p%d" % bu, "gT"])
                        if f1_pend[0] is not None:
                            f1_pend[0]()
                        f1_pend[0] = f1_tail
                if f1_pend[0] is not None:
                    f1_pend[0]()
                    f1_pend[0] = None
                for tt in range(4):
                    i = tb * 4 + tt
                    for hf in range(2):
                        bank = 5 + hf
                        peg([(lambda e, k=k, tt=tt, hf=hf, bank=bank: e.matmul(psum[:, bank, :], lhsT=gT[:, k, tt * 128:(tt + 1) * 128], rhs=wfo[:, k, hf * 512:(hf + 1) * 512], start=(k == 0), stop=(k == NFC - 1))) for k in range(NFC)],
                            r=["gT", "wfoA", "wfoB"], w=["p%d" % bank])
                        hs = slice(hf * 512, (hf + 1) * 512)
                        ev(DVE, lambda e, bank=bank, hs=hs: e.tensor_tensor(out=psum[:, bank, :], in0=psum[:, bank, :], in1=modv[0][:, hs], op=ALU.mult), r=["mod0"], w=["p%d" % bank])
                        ev(DVE, lambda e, bank=bank, hs=hs, i=i: e.tensor_tensor(out=x1[:, i, hs], in0=psum[:, bank, :], in1=x1[:, i, hs], op=ALU.add), w=["p%d" % bank, "x1"])
                    sl_ = slice(i % 16, i % 16 + 1)
                    junk2 = c0s[0].bitcast(BF16)
                    ev(ACT, lambda e, sl_=sl_, junk2=junk2, i=i: e.activation(out=junk2, in_=x1[:, i, :], func=AF.Square, accum_out=ss_t[:, sl_]), r=["x1"], w=["c0_0", "ss%d" % (i % 16)])
                    ev(ACT, lambda e, sl_=sl_: e.activation(out=rs_t[:, sl_], in_=ss_t[:, sl_], func=AF.Sqrt, scale=1.0 / D, bias=EPS), r=["ss%d" % (i % 16)], w=["rs%d" % (i % 16)])
                    ev(DVE, lambda e, sl_=sl_: e.reciprocal(out=rs_t[:, sl_], in_=rs_t[:, sl_]), w=["rs%d" % (i % 16)])
                    ev(DVE, lambda e, sl_=sl_, i=i: e.scalar_tensor_tensor(out=xo, in0=x1[:, i, :], scalar=rs_t[:, sl_], in1=modv[1], op0=ALU.mult, op1=ALU.mult), r=["x1", "rs%d" % (i % 16), "mod1"], w=["xo"])
                    dm(SP, [lambda e, i=i: e.dma_start(out=y_dram.ap()[i * 128:(i + 1) * 128, :], in_=xo)], dsem["st2"], r=["xo"])
            barrier()

        try:
            _chk('K')
            run_group(0, 1024, 4, 256, 0, False, xp, yp)
            _chk('F0')
            run_group(1, 2048, 1, 2048, 256, True, xs, ys)
        except _Stop:
            pass

        fin = [(dsem[n][0], dsem[n][1]) for n in ["st0", "st1", "st2", "st3", "st4"] if dsem[n][1] > 0]
        SP.wait_only(fin)
        for s_ in (PE, ACT, DVE, POOL):
            s_.wait_only(fin)

        @block.tensor
        def _(e):
            PE.emit(e)

        @block.scalar
        def _(e):
            ACT.emit(e)

        @block.vector
        def _(e):
            DVE.emit(e)

        @block.gpsimd
        def _(e):
            POOL.emit(e)

        @block.sync
        def _(e):
            SP.emit(e)
    return nc

def _consts():
    ident = np.eye(128, dtype=np.float32)
    t = np.arange(2048)
    row = (t // 64).astype(np.float32)
    col = (t % 64).astype(np.float32)
    inv = (np.float32(10000.0) ** (-np.arange(16, dtype=np.float32) / np.float32(16))).astype(np.float32)
    cosT = np.zeros((128, 2048), np.float32)
    sinT = np.zeros((128, 2048), np.float32)
    for p in range(128):
        j = p % 64
        a = j // 32
        f = j % 16
        ang = ((row if a == 0 else col) * inv[f]).astype(np.float32)
        cosT[p] = np.cos(ang)
        sinT[p] = np.sin(ang)
    ect = np.ones((4, 16), np.float32)
    for g in range(4):
        w = 2 << g
        hw = w // 2
        for j in range(hw):
            ect[g, j] = 1.0 / (j + hw)
        for j in range(hw - 1):
            ect[g, 8 + j] = 1.0 / (w - 1 - j)
    ect = np.broadcast_to(ect.reshape(1, 64), (128, 64)).copy()
    pm = np.zeros((128, 128), np.float32)
    for m in range(128):
        if (m % 32) < 16:
            pm[m + 16, m] = -1.0
        else:
            pm[m - 16, m] = 1.0
    return ident, cosT, sinT, ect, pm


_NC_CACHE = {}


def kernel(x_prompt, x_sample, cache_k, cache_v, c, c_ctx, w_ada, b_ada, w_in, w_pool, pool_scale,
           w_proj_a, w_proj_b, w_out, lam_q1, lam_k1, lam_q2, lam_k2, subln_w, norm1_w, norm2_w,
           w_ffn_in, ffn_conv_w, ffn_conv_b, w_ffn_out, final_norm_w):
    f = lambda a: np.ascontiguousarray(np.asarray(a, dtype=np.float32))
    if "nc" not in _NC_CACHE:
        _NC_CACHE["nc"] = build_program()
    nc = _NC_CACHE["nc"]
    ident, cosT, sinT, ect, pm = _consts()
    x_prompt = f(x_prompt); x_sample = f(x_sample); cache_k = f(cache_k); cache_v = f(cache_v); c = f(c); c_ctx = f(c_ctx)
    shared = {
        "w_ada": f(w_ada)[0], "b_ada": f(b_ada)[0], "w_in": f(w_in)[0], "w_pool": f(w_pool)[0], "pool_scale": f(pool_scale)[0],
        "w_proj_a": f(w_proj_a)[0], "w_proj_b": f(w_proj_b)[0], "w_out": f(w_out)[0],
        "lamv": np.ascontiguousarray(np.stack([f(lam_q1)[0], f(lam_k1)[0], f(lam_q2)[0], f(lam_k2)[0]], 0)),
        "subln_w": f(subln_w)[0], "norm1_w": f(norm1_w)[0], "norm2_w": f(norm2_w)[0],
        "w_ffn_in": f(w_ffn_in)[0], "conv_w": f(ffn_conv_w)[0], "conv_b": f(ffn_conv_b)[0], "w_ffn_out": f(w_ffn_out)[0],
        "fnw": f(final_norm_w), "c_ident": ident, "c_cos": cosT, "c_sin": sinT, "c_ect": ect, "c_pm": pm,
    }
    in_maps = []
    for i in range(N_CORES):
        m = dict(shared)
        m["xp"] = np.ascontiguousarray(x_prompt[4 * i:4 * i + 4].reshape(1024, D))
        m["xs"] = np.ascontiguousarray(x_sample[i].reshape(2048, D))
        m["ck"] = np.ascontiguousarray(cache_k[i, 0].reshape(256, D))
        m["cv"] = np.ascontiguousarray(cache_v[i, 0].reshape(256, D))
        m["cc"] = np.ascontiguousarray(np.stack([c_ctx, c[i]], 0))
        in_maps.append(m)
    res = run_bass_kernel_spmd(nc, in_maps, core_ids=list(range(N_CORES)))
    r = res.results
    y_prompt = np.concatenate([r[i]["yp"].reshape(4, 256, D) for i in range(N_CORES)], 0)
    y_sample = np.stack([r[i]["ys"].reshape(2048, D) for i in range(N_CORES)], 0)
    new_k = np.concatenate([r[i]["nk"].reshape(4, 1, 256, NH, 2, 64) for i in range(N_CORES)], 0)
    new_v = np.concatenate([r[i]["nv"].reshape(4, 1, 256, NH, 128) for i in range(N_CORES)], 0)
    return (y_prompt.astype(np.float32), y_sample.astype(np.float32), new_k.astype(np.float32), new_v.astype(np.float32))
```
